# Optimizing a Trainium2 kernel written in Bass

```python
import math
import jax, jax.numpy as jnp
from jax import lax
import numpy as np

D_MODEL = 1024
BATCH = 2
SEQ = 8192
DEPTH = 2

HEAD_DIM = 64
ROT_DIM = HEAD_DIM // 4
ROPE_THETA = 500000.0
DIFF_HEADS = 4
DIFF_VDIM = 2 * HEAD_DIM
SB_HEADS = 8
MOBA_HEADS = 8
MOBA_BLOCK = 256
MOBA_TOPK = 3
MOBA_Q_CHUNK = 64
Q_BLOCK = 128
FFN_DIM = 2816
NORM_EPS = 1e-6
DIFF_QK_W = DIFF_HEADS * 2 * HEAD_DIM
DIFF_W = DIFF_HEADS * DIFF_VDIM
SB_W = SB_HEADS * HEAD_DIM
MOBA_W = MOBA_HEADS * HEAD_DIM
IN_SIZES = (DIFF_QK_W, DIFF_QK_W, DIFF_W, SB_W, SB_W, SB_W, MOBA_W, MOBA_W, MOBA_W, D_MODEL, D_MODEL, D_MODEL)
IN_W = 3 * (DIFF_QK_W + DIFF_QK_W // 2) + 3 * SB_W + 3 * MOBA_W - 3 * (DIFF_QK_W // 2) + 3 * D_MODEL - 3 * DIFF_QK_W + DIFF_QK_W * 2 + DIFF_W

kernel_name = "hybrid_diff_stickbreak_moba_macaron"


def rms_norm(x, gain):
    xf = x.astype(jnp.float32)
    y = xf * lax.rsqrt(jnp.mean(xf * xf, axis=-1, keepdims=True) + NORM_EPS)
    return (y * gain.astype(jnp.float32)).astype(x.dtype)


def swiglu(x, wg, wu, wd):
    return (jax.nn.silu(x @ wg) * (x @ wu)) @ wd


def rope_tables(seq_len):
    pos = jnp.arange(seq_len, dtype=jnp.float32)
    inv_freq = ROPE_THETA ** (-jnp.arange(0, ROT_DIM, 2, dtype=jnp.float32) / ROT_DIM)
    ang = pos[:, None] * inv_freq[None, :]
    return jnp.cos(ang), jnp.sin(ang)


def partial_rope(x, cos, sin):
    half = ROT_DIM // 2
    c = cos.astype(x.dtype)
    s = sin.astype(x.dtype)
    x1 = x[..., :half]
    x2 = x[..., half:ROT_DIM]
    return jnp.concatenate([x1 * c - x2 * s, x2 * c + x1 * s, x[..., ROT_DIM:]], axis=-1)


def split_heads(t, n_heads):
    b, s, _ = t.shape
    return t.reshape(b, s, n_heads, -1).transpose(0, 2, 1, 3)


def merge_heads(t):
    b, h, s, d = t.shape
    return t.transpose(0, 2, 1, 3).reshape(b, s, h * d)


def sweep_blocks(fn, n_blocks):
    out = lax.map(fn, jnp.arange(n_blocks))
    n, b, h, blk, d = out.shape
    return out.transpose(1, 2, 0, 3, 4).reshape(b, h, n * blk, d)


def diff_attention(q, k, v, lam, cos, sin):
    seq = q.shape[3]
    q = partial_rope(q, cos, sin)
    k = partial_rope(k, cos, sin)
    scale = HEAD_DIM ** -0.5
    kpos = jnp.arange(seq)
    lam = lam.astype(jnp.float32)

    def block(i):
        q0 = i * Q_BLOCK
        qb = lax.dynamic_slice_in_dim(q, q0, Q_BLOCK, axis=3)
        logits = jnp.einsum('bhmqd,bhmkd->bhmqk', qb, k).astype(jnp.float32) * scale
        qpos = q0 + jnp.arange(Q_BLOCK)
        causal = kpos[None, :] <= qpos[:, None]
        p = jax.nn.softmax(jnp.where(causal, logits, -jnp.inf), axis=-1)
        w = p[:, :, 0] - lam * p[:, :, 1]
        return jnp.einsum('bhqk,bhkd->bhqd', w.astype(v.dtype), v)

    return sweep_blocks(block, seq // Q_BLOCK)


def stick_breaking_attention(q, k, v):
    seq = q.shape[2]
    scale = HEAD_DIM ** -0.5
    kpos = jnp.arange(seq)

    def block(i):
        q0 = i * Q_BLOCK
        qb = lax.dynamic_slice_in_dim(q, q0, Q_BLOCK, axis=2)
        z = jnp.einsum('bhqd,bhkd->bhqk', qb, k).astype(jnp.float32) * scale
        qpos = q0 + jnp.arange(Q_BLOCK)
        strict = kpos[None, :] < qpos[:, None]
        log_keep = jnp.where(strict, jax.nn.log_sigmoid(-z), 0.0)
        shifted = jnp.concatenate([log_keep[..., 1:], jnp.zeros_like(log_keep[..., :1])], axis=-1)
        log_surv = lax.cumsum(shifted, axis=3, reverse=True)
        w = jnp.where(strict, jnp.exp(jax.nn.log_sigmoid(z) + log_surv), 0.0)
        return jnp.einsum('bhqk,bhkd->bhqd', w.astype(v.dtype), v)

    return sweep_blocks(block, seq // Q_BLOCK)


def moba_attention(q, k, v, cos, sin):
    b, h, seq, hd = q.shape
    q = partial_rope(q, cos, sin)
    k = partial_rope(k, cos, sin)
    nb = -(-seq // MOBA_BLOCK)
    pad = nb * MOBA_BLOCK - seq
    kp = jnp.pad(k, ((0, 0), (0, 0), (0, pad), (0, 0)))
    vp = jnp.pad(v, ((0, 0), (0, 0), (0, pad), (0, 0)))
    kblk = kp.reshape(b, h, nb, MOBA_BLOCK, hd)
    vblk = vp.reshape(b, h, nb, MOBA_BLOCK, hd)
    kmean = jnp.mean(kblk, axis=3)
    topk = min(MOBA_TOPK, nb)
    scale = hd ** -0.5
    blk_ids = jnp.arange(nb)
    gather = jax.vmap(jax.vmap(lambda a, idx: a[idx]))

    def chunk(i):
        q0 = i * MOBA_Q_CHUNK
        qb = lax.dynamic_slice_in_dim(q, q0, MOBA_Q_CHUNK, axis=2)
        qpos = q0 + jnp.arange(MOBA_Q_CHUNK)
        own = q0 // MOBA_BLOCK
        gate = jnp.einsum('bhqd,bhnd->bhqn', qb, kmean).astype(jnp.float32)
        gate = jnp.where(blk_ids < own, gate, -jnp.inf)
        _, idx = lax.top_k(gate, topk)
        ksel = gather(kblk, idx)
        vsel = gather(vblk, idx)
        past_ok = jnp.arange(topk) < own
        lp = jnp.einsum('bhqd,bhqrkd->bhqrk', qb, ksel).astype(jnp.float32) * scale
        lp = jnp.where(past_ok[:, None], lp, -jnp.inf).reshape(b, h, MOBA_Q_CHUNK, topk * MOBA_BLOCK)
        kown = lax.dynamic_slice_in_dim(kp, own * MOBA_BLOCK, MOBA_BLOCK, axis=2)
        vown = lax.dynamic_slice_in_dim(vp, own * MOBA_BLOCK, MOBA_BLOCK, axis=2)
        lo = jnp.einsum('bhqd,bhkd->bhqk', qb, kown).astype(jnp.float32) * scale
        own_pos = own * MOBA_BLOCK + jnp.arange(MOBA_BLOCK)
        lo = jnp.where(own_pos[None, :] <= qpos[:, None], lo, -jnp.inf)
        p = jax.nn.softmax(jnp.concatenate([lp, lo], axis=-1), axis=-1)
        pp = p[..., :topk * MOBA_BLOCK].reshape(b, h, MOBA_Q_CHUNK, topk, MOBA_BLOCK).astype(v.dtype)
        po = p[..., topk * MOBA_BLOCK:].astype(v.dtype)
        return (jnp.einsum('bhqrk,bhqrkd->bhqd', pp, vsel)
                + jnp.einsum('bhqk,bhkd->bhqd', po, vown))

    return sweep_blocks(chunk, seq // MOBA_Q_CHUNK)


def hybrid_mixer(h, w_in, w_diff_o, w_sb_o, w_moba_o, w_out,
                 lq1, lk1, lq2, lk2, diff_g, lambda_init, cos, sin):
    b, s, _ = h.shape
    proj = h @ w_in
    offsets = np.cumsum(IN_SIZES)[:-1].tolist()
    (dq, dk, dv, sq, sk, sv, mq, mk, mv, g_diff, g_sb, g_moba) = jnp.split(proj, offsets, axis=-1)

    dq = dq.reshape(b, s, DIFF_HEADS, 2, HEAD_DIM).transpose(0, 2, 3, 1, 4)
    dk = dk.reshape(b, s, DIFF_HEADS, 2, HEAD_DIM).transpose(0, 2, 3, 1, 4)
    lam = (jnp.exp(jnp.sum(lq1.astype(jnp.float32) * lk1.astype(jnp.float32)))
           - jnp.exp(jnp.sum(lq2.astype(jnp.float32) * lk2.astype(jnp.float32))) + lambda_init)
    a_out = diff_attention(dq, dk, split_heads(dv, DIFF_HEADS), lam, cos, sin)
    a_out = merge_heads(rms_norm(a_out, diff_g) * (1.0 - lambda_init))

    b_out = merge_heads(stick_breaking_attention(split_heads(sq, SB_HEADS),
                                                 split_heads(sk, SB_HEADS),
                                                 split_heads(sv, SB_HEADS)))
    c_out = merge_heads(moba_attention(split_heads(mq, MOBA_HEADS),
                                       split_heads(mk, MOBA_HEADS),
                                       split_heads(mv, MOBA_HEADS), cos, sin))

    merged = (jax.nn.sigmoid(g_diff) * (a_out @ w_diff_o)
              + jax.nn.sigmoid(g_sb) * (b_out @ w_sb_o)
              + jax.nn.sigmoid(g_moba) * (c_out @ w_moba_o))
    return merged @ w_out


def setup_inputs(seed: int = 0) -> dict:
    key = jax.random.key(seed)
    ks = jax.random.split(key, 24)
    f32 = jnp.float32

    def nrm(k, shape, fan_in):
        return jax.random.normal(k, shape, f32) * (fan_in ** -0.5)

    def gain(k, shape):
        return 1.0 + 0.02 * jax.random.normal(k, shape, f32)

    L, D = DEPTH, D_MODEL
    return {
        "x": jax.random.normal(ks[0], (BATCH, SEQ, D), f32),
        "w_in": nrm(ks[1], (L, D, IN_W), D),
        "w_diff_o": nrm(ks[2], (L, DIFF_W, D), DIFF_W),
        "w_sb_o": nrm(ks[3], (L, SB_W, D), SB_W),
        "w_moba_o": nrm(ks[4], (L, MOBA_W, D), MOBA_W),
        "w_out": nrm(ks[5], (L, D, D), D),
        "lam_q1": 0.1 * jax.random.normal(ks[6], (L, HEAD_DIM), f32),
        "lam_k1": 0.1 * jax.random.normal(ks[7], (L, HEAD_DIM), f32),
        "lam_q2": 0.1 * jax.random.normal(ks[8], (L, HEAD_DIM), f32),
        "lam_k2": 0.1 * jax.random.normal(ks[9], (L, HEAD_DIM), f32),
        "diff_norm_g": gain(ks[10], (L, DIFF_VDIM)),
        "ffn1_wg": nrm(ks[11], (L, D, FFN_DIM), D),
        "ffn1_wu": nrm(ks[12], (L, D, FFN_DIM), D),
        "ffn1_wd": nrm(ks[13], (L, FFN_DIM, D), FFN_DIM),
        "ffn2_wg": nrm(ks[14], (L, D, FFN_DIM), D),
        "ffn2_wu": nrm(ks[15], (L, D, FFN_DIM), D),
        "ffn2_wd": nrm(ks[16], (L, FFN_DIM, D), FFN_DIM),
        "g_ffn1_pre": gain(ks[17], (L, D)),
        "g_ffn1_post": gain(ks[18], (L, D)),
        "g_mix_pre": gain(ks[19], (L, D)),
        "g_mix_post": gain(ks[20], (L, D)),
        "g_ffn2_pre": gain(ks[21], (L, D)),
        "g_ffn2_post": gain(ks[22], (L, D)),
    }


def reference(x, w_in, w_diff_o, w_sb_o, w_moba_o, w_out, lam_q1, lam_k1, lam_q2, lam_k2,
              diff_norm_g, ffn1_wg, ffn1_wu, ffn1_wd, ffn2_wg, ffn2_wu, ffn2_wd,
              g_ffn1_pre, g_ffn1_post, g_mix_pre, g_mix_post, g_ffn2_pre, g_ffn2_post):
    cos, sin = rope_tables(x.shape[1])
    for l in range(DEPTH):
        lambda_init = 0.8 - 0.6 * math.exp(-0.3 * l)
        x = x + 0.5 * rms_norm(swiglu(rms_norm(x, g_ffn1_pre[l]), ffn1_wg[l], ffn1_wu[l], ffn1_wd[l]),
                               g_ffn1_post[l])
        mixed = hybrid_mixer(rms_norm(x, g_mix_pre[l]), w_in[l], w_diff_o[l], w_sb_o[l], w_moba_o[l],
                             w_out[l], lam_q1[l], lam_k1[l], lam_q2[l], lam_k2[l], diff_norm_g[l],
                             lambda_init, cos, sin)
        x = x + rms_norm(mixed, g_mix_post[l])
        x = x + 0.5 * rms_norm(swiglu(rms_norm(x, g_ffn2_pre[l]), ffn2_wg[l], ffn2_wu[l], ffn2_wd[l]),
                               g_ffn2_post[l])
    return x
```

```python
import math
import contextlib
import numpy as np
import ml_dtypes
import concourse.bass as bass
import concourse.mybir as mybir
from concourse.bass_utils import run_bass_kernel_spmd

F32 = mybir.dt.float32
BF16 = mybir.dt.bfloat16
I32 = mybir.dt.int32
AF = mybir.ActivationFunctionType
ALU = mybir.AluOpType
AX = mybir.AxisListType

ENGS = ["tensor", "vector", "scalar", "gpsimd", "sync"]


class Op:
    __slots__ = ("eng", "fn", "dma", "deps", "idx", "ticket", "needs_inc", "dsem", "dval", "dprev", "cc")

    def __init__(self, eng, fn, dma, idx):
        self.eng = eng
        self.fn = fn
        self.dma = dma
        self.deps = ()
        self.idx = idx
        self.ticket = None
        self.needs_inc = False
        self.dsem = None
        self.dval = None
        self.dprev = None
        self.cc = None


class Sched:
    def __init__(self, nc, n_dma_sems=28):
        self.nc = nc
        self.ops = []
        self.last_w = {}
        self.readers = {}
        self.n_dma_sems = n_dma_sems
        self.stack = contextlib.ExitStack()
        self.pstack = None
        self.n_dma = 0
        self.n_dma_by = {}
        self.n_cc = 0
        self.since_barrier = []

    def sb(self, name, shape, dtype):
        st = self.pstack if self.pstack is not None else self.stack
        if self.pstack is not None:
            name = "%s_ph%d" % (name, self.n_phase)
        return st.enter_context(self.nc.sbuf_tensor(name, list(shape), dtype))

    def phase_begin(self):
        self.n_phase = getattr(self, "n_phase", 0) + 1
        self.pstack = contextlib.ExitStack()

    def phase_end(self):
        self.barrier()
        self.pstack.close()
        self.pstack = None

    def barrier(self):
        prev = list(self.since_barrier)
        self.since_barrier = []
        last = {}
        deps = []
        for o in prev:
            if o.dma:
                deps.append(o)
            elif o.fn is not None:
                last[o.eng] = o
        deps += list(last.values())
        for e in ENGS:
            j = Op(e, None, False, len(self.ops))
            j.deps = list(deps)
            self.ops.append(j)
        self.last_w = {}
        self.readers = {}

    def collective(self, fn, writes=()):
        o = Op("gpsimd", fn, True, len(self.ops))
        o.deps = []
        for k in writes:
            self.last_w[k] = o
            self.readers[k] = []
        o.cc = self.n_cc
        self.n_cc += 1
        self.ops.append(o)
        self.since_barrier.append(o)
        return o

    def ps(self, name, shape, dtype=F32):
        return self.stack.enter_context(self.nc.psum_tensor(name, list(shape), dtype))

    def op(self, eng, fn, reads=(), writes=(), dma=False):
        o = Op(eng, fn, dma, len(self.ops))
        deps = {}
        for k in reads:
            w = self.last_w.get(k)
            if w is not None:
                deps[w.idx] = w
            if isinstance(k, tuple) and k and k[0] == "ps" or (isinstance(k, str) and k.startswith("ps")):
                for r in self.readers.get(k, ()):
                    if r.eng != eng:
                        deps[r.idx] = r
        for k in writes:
            w = self.last_w.get(k)
            if w is not None:
                deps[w.idx] = w
            for r in self.readers.get(k, ()):
                deps[r.idx] = r
        deps.pop(o.idx, None)
        o.deps = list(deps.values())
        for k in writes:
            self.last_w[k] = o
            self.readers[k] = []
        for k in reads:
            self.readers.setdefault(k, []).append(o)
        if dma:
            n = self.n_dma_by.get(eng, 0)
            o.dsem = (eng, n % self.n_dma_sems)
            o.dval = 16 * (n // self.n_dma_sems + 1)
            self.n_dma_by[eng] = n + 1
            self.n_dma += 1
        self.ops.append(o)
        self.since_barrier.append(o)
        return o

    def dma(self, eng, out, in_, reads=(), writes=(), **kw):
        return self.op(eng, lambda e: e.dma_start(out=out, in_=in_, **kw), reads, writes, dma=True)

    def join(self, eng, ops):
        o = Op(eng, None, False, len(self.ops))
        o.deps = list(ops)
        self.ops.append(o)
        return o

    def emit(self):
        nc = self.nc
        for o in self.ops:
            for d in o.deps:
                if not d.dma and d.fn is not None:
                    if d.eng == o.eng and o.eng == "tensor" and not o.dma:
                        continue
                    d.needs_inc = True
        cnt = {e: 0 for e in ENGS}
        for o in self.ops:
            if (not o.dma) and o.needs_inc:
                cnt[o.eng] += 1
                o.ticket = cnt[o.eng]
        esem = {e: self.stack.enter_context(nc.semaphore("es_" + e)) for e in ENGS}
        dsem = {}
        for en in self.n_dma_by:
            for i in range(min(self.n_dma_sems, self.n_dma_by[en])):
                dsem[(en, i)] = self.stack.enter_context(nc.semaphore("ds_%s_%d" % (en, i)))
        csem = [self.stack.enter_context(nc.semaphore("cs_%d" % i)) for i in range(self.n_cc)]
        by_eng = {e: [o for o in self.ops if o.eng == e] for e in ENGS}
        self.stats = {e: len(by_eng[e]) for e in ENGS}

        def run(engname, e):
            waited = {}

            def wait(key, sem, val):
                if waited.get(key, 0) >= val:
                    return
                waited[key] = val
                e.wait_ge(sem, val)

            for o in by_eng[engname]:
                for d in o.deps:
                    if d.cc is not None:
                        wait(("c", d.cc), csem[d.cc], 1)
                    elif d.dma:
                        wait(("d", d.dsem), dsem[d.dsem], d.dval)
                    else:
                        if d.eng == engname and engname == "tensor" and not o.dma:
                            continue
                        wait(("e", d.eng), esem[d.eng], d.ticket)
                if o.dma and o.cc is None and o.dval > 16:
                    wait(("d", o.dsem), dsem[o.dsem], o.dval - 16)
                if o.fn is None:
                    continue
                ins = o.fn(e)
                if o.cc is not None:
                    ins.then_inc(csem[o.cc])
                elif o.dma:
                    ins.then_inc(dsem[o.dsem], 16)
                elif o.needs_inc:
                    ins.then_inc(esem[engname], 1)

        with nc.Block() as block:
            @block.tensor
            def _(e):
                run("tensor", e)

            @block.vector
            def _(e):
                run("vector", e)

            @block.scalar
            def _(e):
                run("scalar", e)

            @block.gpsimd
            def _(e):
                run("gpsimd", e)

            @block.sync
            def _(e):
                run("sync", e)

    def close(self):
        if self.pstack is not None:
            self.pstack.close()
            self.pstack = None
        self.stack.close()


D = 1024
NC8 = 8
NF = 22
TOK = 2048
EPS = 1e-6
S = 8192
NKC = S // 128
BIG = 30000.0
SEND_ROWS = 3072


class Ctx:
    pass


def rot(ctx, name, n):
    i = ctx.cnt.get(name, 0)
    ctx.cnt[name] = i + 1
    return i % n


def pipeline(items, stages, lags, hook=None):
    n = len(items)
    for k in range(n + max(lags)):
        if hook is not None and k == n // 2:
            hook()
        for st, lag in zip(stages, lags):
            i = k - lag
            if 0 <= i < n:
                st(items[i])


class TokStage:
    def __init__(self, ctx, TG=1024):
        self.ctx = ctx
        self.nc, self.s, self.PS = ctx.nc, ctx.s, ctx.PS
        s = self.s
        self.TG = TG
        self.NT = TG // 512
        self.ones = ctx.ones
        self.epsc = ctx.epsc
        self.xg = s.sb("xg", [128, NC8, TG], F32)
        self.hb = s.sb("hb", [128, NC8, TG], BF16)
        self.act = s.sb("act", [128, NF, TG], BF16)
        self.yt = s.sb("yt", [128, NC8, TG], F32)
        self.rstd = s.sb("rstd", [128, TG], F32)
        self.sq = [s.sb("sq%d" % i, [128, 512], BF16) for i in range(2)]
        self.sg = [s.sb("sg%d" % i, [128, 512], F32) for i in range(2)]
        self.wA = [s.sb("wA%d" % i, [128, NC8, 128], BF16) for i in range(8)]
        self.wD = [s.sb("wD%d" % i, [128, NF, 128], BF16) for i in range(3)]
        self.gains = {}
        self.proj_init = False
        self.merge_init = False

    def rot(self, name, n):
        return rot(self.ctx, name, n)

    def load_gain(self, name, dram, half=False):
        s = self.s
        t = s.sb("g_" + name, [128, NC8], F32)
        s.dma("sync", t[:], dram, writes=["g_" + name])
        if half:
            s.op("vector", lambda e: e.tensor_scalar(out=t[:], in0=t[:], scalar1=0.5, scalar2=None, op0=ALU.mult),
                 reads=["g_" + name], writes=["g_" + name])
        self.gains[name] = t

    def rms_stats(self, src, srckey):
        s = self.s
        for nt in range(self.NT):
            sl = slice(nt * 512, (nt + 1) * 512)
            bank = 6
            ps = self.PS[bank]
            for c in range(NC8):
                qi = self.rot("sq", 2)
                sq = self.sq[qi]
                s.op("scalar", lambda e, sq=sq, c=c, sl=sl: e.activation(out=sq[:], in_=src[:, c, sl], func=AF.Square),
                     reads=[(srckey, c, nt)], writes=[("sq", qi)])
                s.op("tensor", lambda e, sq=sq, c=c, ps=ps: e.matmul(ps[:], lhsT=self.ones[:], rhs=sq[:], start=(c == 0), stop=(c == NC8 - 1)),
                     reads=["ones", ("sq", qi)], writes=[("ps", bank)])
            rs = self.rstd
            s.op("scalar", lambda e, ps=ps, sl=sl: e.activation(out=rs[:, sl], in_=ps[:], func=AF.Ln, bias=self.epsc[:, 0:1], scale=1.0 / D),
                 reads=[("ps", bank), "epsc"], writes=[("rstd", nt)])
            s.op("scalar", lambda e, sl=sl: e.activation(out=rs[:, sl], in_=rs[:, sl], func=AF.Exp, scale=-0.5), reads=[("rstd", nt)], writes=[("rstd", nt)])

    def norm_to_hb(self, gname):
        s = self.s
        g = self.gains[gname]
        self.rms_stats(self.xg, "xg")
        for nt in range(self.NT):
            sl = slice(nt * 512, (nt + 1) * 512)
            for c in range(NC8):
                s.op("vector", lambda e, c=c, sl=sl: e.scalar_tensor_tensor(out=self.hb[:, c, sl], in0=self.xg[:, c, sl], scalar=g[:, c:c + 1],
                                                                             in1=self.rstd[:, sl], op0=ALU.mult, op1=ALU.mult),
                     reads=[("xg", c, nt), ("rstd", nt), "g_" + gname], writes=[("hb", c, nt)])

    def evac_stats(self, bank, j, nt):
        s = self.s
        sl = slice(nt * 512, (nt + 1) * 512)
        s.op("scalar", lambda e: e.activation(out=self.yt[:, j, sl], in_=self.PS[bank][:], func=AF.Copy),
             reads=[("ps", bank)], writes=[("yt", j, nt)])
        qi = self.rot("sq", 2)
        sq = self.sq[qi]
        s.op("scalar", lambda e: e.activation(out=sq[:], in_=self.PS[bank][:], func=AF.Square),
             reads=[("ps", bank)], writes=[("sq", qi)])
        self.flush_stats()
        self.pending = (sq, qi, j, nt)

    def flush_stats(self):
        if getattr(self, "pending", None) is None:
            return
        sq, qi, j, nt = self.pending
        self.pending = None
        sb_ = 6 + nt
        self.s.op("tensor", lambda e: e.matmul(self.PS[sb_][:], lhsT=self.ones[:], rhs=sq[:], start=(j == 0), stop=(j == NC8 - 1)),
                  reads=["ones", ("sq", qi)], writes=[("ps", sb_)])

    def post_norm_residual(self, gname):
        s = self.s
        g = self.gains[gname]
        self.flush_stats()
        rs = self.rstd
        for nt in range(self.NT):
            sl = slice(nt * 512, (nt + 1) * 512)
            sb_ = 6 + nt
            s.op("scalar", lambda e, sb_=sb_, sl=sl: e.activation(out=rs[:, sl], in_=self.PS[sb_][:], func=AF.Ln, bias=self.epsc[:, 0:1], scale=1.0 / D),
                 reads=[("ps", sb_), "epsc"], writes=[("rstd", nt)])
            s.op("scalar", lambda e, sl=sl: e.activation(out=rs[:, sl], in_=rs[:, sl], func=AF.Exp, scale=-0.5), reads=[("rstd", nt)], writes=[("rstd", nt)])
        for nt in range(self.NT):
            sl = slice(nt * 512, (nt + 1) * 512)
            for c in range(NC8):
                s.op("vector", lambda e, c=c, sl=sl: e.scalar_tensor_tensor(out=self.yt[:, c, sl], in0=self.yt[:, c, sl], scalar=g[:, c:c + 1],
                                                                             in1=self.rstd[:, sl], op0=ALU.mult, op1=ALU.mult),
                     reads=[("yt", c, nt), ("rstd", nt), "g_" + gname], writes=[("yt", c, nt)])
                s.op("vector", lambda e, c=c, sl=sl: e.tensor_tensor(out=self.xg[:, c, sl], in0=self.xg[:, c, sl], in1=self.yt[:, c, sl], op=ALU.add),
                     reads=[("yt", c, nt), ("xg", c, nt)], writes=[("xg", c, nt)])

    def load_w(self, dram_ap, nk):
        s = self.s
        if nk <= NC8:
            i = self.rot("wA", 8)
            t = self.wA[i]
            key = ("wA", i)
        else:
            i = self.rot("wD", 3)
            t = self.wD[i]
            key = ("wD", i)
        s.dma("gpsimd", t[:, 0:nk, :], dram_ap, writes=[key])
        return t, key

    def mm_group(self, bank, wt, wkey, nk, rhs_fn, rhs_keys):
        ps = self.PS[bank]

        def fn(e):
            ins = None
            for kc in range(nk):
                ins = e.matmul(ps[:], lhsT=wt[:, kc, :], rhs=rhs_fn(kc), start=(kc == 0), stop=(kc == nk - 1))
            return ins
        return self.s.op("tensor", fn, reads=[wkey] + list(rhs_keys), writes=[("ps", bank)])

    def ffn(self, W, g_pre, g_post_half):
        s = self.s
        wg, wu, wd = W
        self.norm_to_hb(g_pre)
        for fc in range(NF):
            wgt, wgk = self.load_w(wg[fc], NC8)
            wut, wuk = self.load_w(wu[fc], NC8)
            for nt in range(self.NT):
                sl = slice(nt * 512, (nt + 1) * 512)
                par = self.rot("gu", 2)
                bg, bu = 0 + par, 2 + par
                hkeys = [("hb", c, nt) for c in range(NC8)]
                self.mm_group(bg, wgt, wgk, NC8, lambda kc, sl=sl: self.hb[:, kc, sl], hkeys)
                self.mm_group(bu, wut, wuk, NC8, lambda kc, sl=sl: self.hb[:, kc, sl], hkeys)
                sgi = self.rot("sg", 2)
                sg = self.sg[sgi]
                s.op("scalar", lambda e, sg=sg, bg=bg: e.activation(out=sg[:], in_=self.PS[bg][:], func=AF.Silu),
                     reads=[("ps", bg)], writes=[("sg", sgi)])
                s.op("vector", lambda e, sg=sg, bu=bu, fc=fc, sl=sl: e.tensor_tensor(out=self.act[:, fc, sl], in0=sg[:], in1=self.PS[bu][:], op=ALU.mult),
                     reads=[("sg", sgi), ("ps", bu)], writes=[("act", fc, nt)])
        for j in range(NC8):
            wdt, wdk = self.load_w(wd[j], NF)
            for nt in range(self.NT):
                sl = slice(nt * 512, (nt + 1) * 512)
                bank = 4 + self.rot("dn", 2)
                self.mm_group(bank, wdt, wdk, NF, lambda kc, sl=sl: self.act[:, kc, sl], [("act", fc, nt) for fc in range(NF)])
                self.evac_stats(bank, j, nt)
        self.post_norm_residual(g_post_half)

    def proj(self, P, g_pre, t0):
        s = self.s
        TG, NT = self.TG, self.NT
        if not self.proj_init:
            self.proj_init = True
            self.cos = s.sb("cos", [128, TG], F32)
            self.sin = s.sb("sin", [128, TG], F32)
            self.pm = s.sb("pm", [128, 128], BF16)
            s.dma("sync", self.pm[:], P["pm"], writes=["pm"])
            self.wV = s.sb("wV", [128, NC8, 512], BF16)
            self.stb = [s.sb("stb%d" % i, [128, 512], BF16) for i in range(3)]
            self.stf = [s.sb("stf%d" % i, [128, 512], F32) for i in range(3)]
        s.dma("sync", self.cos[:], P["cos"][:, t0:t0 + TG], writes=["cos"])
        s.dma("sync", self.sin[:], P["sin"][:, t0:t0 + TG], writes=["sin"])
        self.norm_to_hb(g_pre)
        ROPE = set(range(0, 4)) | set(range(8, 16)) | set(range(20, 24))
        for oc in range(48):
            wt, wk = self.load_w(P["w_fm"][oc], NC8)
            for nt in range(NT):
                sl = slice(nt * 512, (nt + 1) * 512)
                gsl = slice(t0 + nt * 512, t0 + (nt + 1) * 512)
                bank = self.rot("pj", 2)
                ps = self.PS[bank]
                hkeys = [("hb", c, nt) for c in range(NC8)]
                self.mm_group(bank, wt, wk, NC8, lambda kc, sl=sl: self.hb[:, kc, sl], hkeys)
                if oc >= 24:
                    fi = self.rot("stf", 3)
                    st = self.stf[fi]
                    s.op("scalar", lambda e, st=st, ps=ps: e.activation(out=st[:], in_=ps[:], func=AF.Sigmoid),
                         reads=[("ps", bank)], writes=[("stf", fi)])
                    s.dma("sync", P["g_dst"][oc - 24, :, gsl], st[:], reads=[("stf", fi)])
                    continue
                if oc < 12:
                    dst = P["q_dst"][oc, :, gsl]
                else:
                    dst = P["send"][oc - 12][0:128, gsl]
                if oc not in ROPE:
                    bi = self.rot("stb", 3)
                    st = self.stb[bi]
                    s.op("scalar", lambda e, st=st, ps=ps: e.activation(out=st[:], in_=ps[:], func=AF.Copy),
                         reads=[("ps", bank)], writes=[("stb", bi)])
                    s.dma("sync", dst, st[:], reads=[("stb", bi)])
                else:
                    bi = self.rot("stb", 3)
                    xb = self.stb[bi]
                    s.op("scalar", lambda e, xb=xb, ps=ps: e.activation(out=xb[:], in_=ps[:], func=AF.Copy),
                         reads=[("ps", bank)], writes=[("stb", bi)])
                    b2 = 2 + self.rot("pj2", 2)
                    ps2 = self.PS[b2]
                    s.op("tensor", lambda e, ps2=ps2, xb=xb: e.matmul(ps2[:], lhsT=self.pm[:], rhs=xb[:], start=True, stop=True),
                         reads=["pm", ("stb", bi)], writes=[("ps", b2)])
                    fi = self.rot("stf", 3)
                    t1 = self.stf[fi]
                    s.op("vector", lambda e, t1=t1, ps=ps, sl=sl: e.tensor_tensor(out=t1[:], in0=ps[:], in1=self.cos[:, sl], op=ALU.mult),
                         reads=[("ps", bank), "cos"], writes=[("stf", fi)])
                    fj = self.rot("stf", 3)
                    t2 = self.stf[fj]
                    s.op("vector", lambda e, t2=t2, ps2=ps2, sl=sl: e.tensor_tensor(out=t2[:], in0=ps2[:], in1=self.sin[:, sl], op=ALU.mult),
                         reads=[("ps", b2), "sin"], writes=[("stf", fj)])
                    bo = self.rot("stb", 3)
                    ob = self.stb[bo]
                    s.op("vector", lambda e, ob=ob, t1=t1, t2=t2: e.tensor_tensor(out=ob[:], in0=t1[:], in1=t2[:], op=ALU.add),
                         reads=[("stf", fi), ("stf", fj)], writes=[("stb", bo)])
                    s.dma("sync", dst, ob[:], reads=[("stb", bo)])
        for vg in range(3):
            s.dma("gpsimd", self.wV[:], P["w_v"][vg], writes=["wV"])
            for tt in range(TG // 128):
                tsl = slice(tt * 128, (tt + 1) * 128)
                cc = t0 // 128 + tt
                bank = 4 + self.rot("pv", 2)
                ps = self.PS[bank]

                def fn(e, ps=ps, tsl=tsl):
                    ins = None
                    for kc in range(NC8):
                        ins = e.matmul(ps[:], lhsT=self.hb[:, kc, tsl], rhs=self.wV[:, kc, :], start=(kc == 0), stop=(kc == NC8 - 1))
                    return ins
                s.op("tensor", fn, reads=["wV"] + [("hb", c, tt // 4) for c in range(NC8)], writes=[("ps", bank)])
                bi = self.rot("stb", 3)
                st = self.stb[bi]
                s.op("scalar", lambda e, st=st, ps=ps: e.activation(out=st[:], in_=ps[:], func=AF.Copy),
                     reads=[("ps", bank)], writes=[("stb", bi)])
                for hh in range(4):
                    dst = P["send"][vg * 4 + hh][128:256, cc * 128:(cc + 1) * 128]
                    s.dma("sync", dst, st[:, hh * 128:(hh + 1) * 128], reads=[("stb", bi)])

    def merge(self, M, g_post, t0):
        s = self.s
        TG, NT = self.TG, self.NT
        if not self.merge_init:
            self.merge_init = True
            self.gt = [s.sb("gt%d" % i, [128, 512], F32) for i in range(3)]
            self.mt = [s.sb("mt%d" % i, [128, 512], F32) for i in range(2)]
        s.dma("sync", self.act[:, 0:12, :], M["a_src"][:, :, t0:t0 + TG].rearrange("c p t -> p c t"),
              writes=[("act", c, nt) for c in range(12) for nt in range(NT)])
        for j in range(NC8):
            wts = [self.load_w(M["w_mo"][i, j], 4) for i in range(3)]
            for nt in range(NT):
                sl = slice(nt * 512, (nt + 1) * 512)
                gsl = slice(t0 + nt * 512, t0 + (nt + 1) * 512)
                mi = self.rot("mt", 2)
                macc = self.mt[mi]
                for i in range(3):
                    wt, wk = wts[i]
                    bank = self.rot("mg", 3)
                    ps = self.PS[bank]
                    self.mm_group(bank, wt, wk, 4, lambda kc, sl=sl, i=i: self.act[:, 4 * i + kc, sl], [("act", 4 * i + kc, nt) for kc in range(4)])
                    gi = self.rot("gt", 3)
                    gt = self.gt[gi]
                    s.dma("sync", gt[:], M["g_src"][8 * i + j, :, gsl], writes=[("gt", gi)])
                    if i == 0:
                        s.op("vector", lambda e, macc=macc, ps=ps, gt=gt: e.tensor_tensor(out=macc[:], in0=ps[:], in1=gt[:], op=ALU.mult),
                             reads=[("ps", bank), ("gt", gi)], writes=[("mt", mi)])
                    else:
                        s.op("vector", lambda e, ps=ps, gt=gt: e.tensor_tensor(out=gt[:], in0=ps[:], in1=gt[:], op=ALU.mult),
                             reads=[("ps", bank), ("gt", gi)], writes=[("gt", gi)])
                        if i == 1:
                            s.op("vector", lambda e, macc=macc, gt=gt: e.tensor_tensor(out=macc[:], in0=macc[:], in1=gt[:], op=ALU.add),
                                 reads=[("mt", mi), ("gt", gi)], writes=[("mt", mi)])
                        else:
                            s.op("vector", lambda e, macc=macc, gt=gt, j=j, sl=sl: e.tensor_tensor(out=self.hb[:, j, sl], in0=macc[:], in1=gt[:], op=ALU.add),
                                 reads=[("mt", mi), ("gt", gi)], writes=[("hb", j, nt)])
        for j in range(NC8):
            wt, wk = self.load_w(M["w_out"][j], NC8)
            for nt in range(NT):
                sl = slice(nt * 512, (nt + 1) * 512)
                bank = 4 + self.rot("dn", 2)
                self.mm_group(bank, wt, wk, NC8, lambda kc, sl=sl: self.hb[:, kc, sl], [("hb", c, nt) for c in range(NC8)])
                self.evac_stats(bank, j, nt)
        self.post_norm_residual(g_post)

    def run(self, x_src, x_dst, merge=None, ffns=(), proj=None, final=None):
        s = self.s
        allx = [("xg", c, nt) for c in range(NC8) for nt in range(self.NT)]
        for g in range(TOK // self.TG):
            t0 = g * self.TG
            s.dma("sync", self.xg[:], x_src[:, :, t0:t0 + self.TG], writes=allx)
            if merge is not None:
                self.merge(merge[0], merge[1], t0)
            for W, gp, gq in ffns:
                self.ffn(W, gp, gq)
            if proj is not None:
                self.proj(proj[0], proj[1], t0)
            d = s.dma("sync", x_dst[:, :, t0:t0 + self.TG], self.xg[:], reads=allx)
            if final is not None:
                final.append(d)


class AttnStage:
    def __init__(self, ctx, q_loc, gath, o_dst, lam_aps):
        self.ctx = ctx
        self.nc, self.s, self.PS = ctx.nc, ctx.s, ctx.PS
        s = self.s
        self.q_loc, self.gath, self.o_dst = q_loc, gath, o_dst
        self.c = {}
        for name, (d, shape, dty) in ctx.acon_d.items():
            t = s.sb("c_" + name, shape, dty)
            s.dma("sync", t[:], d, writes=[name])
            self.c[name] = t
        self.sets = []
        for si in range(2):
            B = Ctx()
            B.QA = s.sb("QA%d" % si, [128, TOK], BF16)
            B.KA = s.sb("KA%d" % si, [128, S], BF16)
            B.KB = s.sb("KB%d" % si, [128, S], BF16)
            B.V = s.sb("V%d" % si, [128, NKC, 128], BF16)
            B.kQA, B.kKA, B.kKB, B.kV, B.si = ("QA", si), ("KA", si), ("KB", si), ("V", si), si
            s.op("gpsimd", lambda e, B=B: e.memset(B.KA[64:128, :], 0.0), writes=[B.kKA])
            s.op("gpsimd", lambda e, B=B: e.memset(B.KB[0:64, :], 0.0), writes=[B.kKB])
            self.sets.append(B)
        self.Pb = [s.sb("Pb%d" % i, [128, 512], BF16) for i in range(8)]
        self.E = [s.sb("E%d" % i, [128, 512], F32) for i in range(8)]
        self.sqb = s.sb("sqb", [128, 512], BF16)
        self.F = [s.sb("F%d" % i, [128, 512], F32) for i in range(6)]
        self.ob = [s.sb("ob%d" % i, [128, 512], BF16) for i in range(2)]
        self.carry = s.sb("carry", [128, 512], F32)
        self.carry2 = s.sb("carry2", [128, 512], F32)
        self.sm = s.sb("sm", [128, 64], F32)
        self.lam = s.sb("lam", [128, 8], F32)
        self.lamv = s.sb("lamv", [128, 4, 64], F32)
        self.lami = s.sb("lami", [128, 1], F32)
        self.dng = s.sb("dng", [128, 1], F32)
        s.dma("sync", self.lamv[:], lam_aps[0], writes=["lamv"])
        s.dma("sync", self.lami[:], lam_aps[1], writes=["lam_init"])
        s.dma("sync", self.dng[:], lam_aps[2], writes=["dng"])
        self.km = s.sb("km", [128, 32], F32)
        self.kmb = s.sb("kmb", [128, 32], BF16)
        self.gm = s.sb("gm", [128, 512], F32)
        self.btall = s.sb("btall", [128, 512], F32)
        self.mxall = s.sb("mxall", [128, 16, 8], F32)

        self.lam_setup()
        units = []
        for hh in range(4):
            units.append((lambda B, hh=hh: self.load_qkv(B, hh, hh, 0, hh), None, lambda B, hook, hh=hh: self.diff(B, hh, hook)))
        for hh in range(4):
            units.append((lambda B, hh=hh: self.load_qkv(B, 4 + hh, 4 + hh, 1, hh), None,
                          lambda B, hook, hh=hh: (self.sb2(B, hh) if hook is None else (self.sb(B, hh, 0, None), self.sb(B, hh, 1, hook)))))
        for hh in range(4):
            for h in range(2):
                units.append((lambda B, hh=hh, h=h: self.moba_load(B, hh, h), lambda B, hh=hh, h=h: self.moba_pre(B, hh, h),
                              lambda B, hook, hh=hh, h=h: self.moba(B, hh, h, hook)))
        units[0][0](self.sets[0])
        for n, (ld, pre, comp) in enumerate(units):
            hook = None
            if n + 1 < len(units):
                units[n + 1][0](self.sets[(n + 1) % 2])
                if units[n + 1][1] is not None:
                    hook = (lambda n=n: units[n + 1][1](self.sets[(n + 1) % 2]))
            comp(self.sets[n % 2], hook)

    def rot(self, name, n):
        return rot(self.ctx, name, n)

    def load_k(self, kchunk, dst_rows, src_rows, dst_t, key):
        s = self.s
        for r in range(4):
            base = r * 256
            src = self.gath[kchunk][base + src_rows.start:base + src_rows.stop, :].rearrange("p (i t) -> p i t", t=512)
            dst = dst_t[dst_rows, :].rearrange("p (i r t) -> p i r t", r=4, t=512)[:, :, r, :]
            s.dma("sync", dst, src, reads=[("gath", kchunk)], writes=[key])

    def load_v(self, B, vg, hh):
        s = self.s
        u = vg * 4 + hh
        for r in range(4):
            base = r * 256 + 128
            src = self.gath[u][base:base + 128, :].rearrange("p (i j d) -> p i j d", j=4, d=128)
            dst = B.V[:, :, :].rearrange("p (i r j) d -> p i r j d", r=4, j=4)[:, :, r, :, :]
            s.dma("sync", dst, src, reads=[("gath", u)], writes=[B.kV])

    def load_qkv(self, B, qchunk, kchunk_global, vg, hh):
        s = self.s
        s.dma("sync", B.QA[:, :], self.q_loc[qchunk, :, :], writes=[B.kQA])
        self.load_k(kchunk_global, slice(0, 64), slice(0, 64), B.KA, B.kKA)
        self.load_k(kchunk_global, slice(64, 128), slice(64, 128), B.KB, B.kKB)
        self.load_v(B, vg, hh)

    def moba_load(self, B, hh, h):
        s = self.s
        hp = slice(64 * h, 64 * h + 64)
        hi = slice(64, 128)
        s.op("gpsimd", lambda e: e.memset(B.KA[32:64, :], 0.0), writes=[B.kKA])
        s.op("gpsimd", lambda e: e.memset(B.QA[32:64, :], 0.0), writes=[B.kQA])
        s.dma("sync", B.KA[0:32, :], self.ctx.blk1h_d, writes=[B.kKA])
        s.dma("sync", B.QA[hi, :], self.q_loc[8 + hh, hp, :], writes=[B.kQA])
        self.load_k(8 + hh, hi, hp, B.KA, B.kKA)
        self.load_v(B, 2, hh)

    def lam_setup(self):
        s = self.s
        lamv, sm, lam = self.lamv, self.sm, self.lam
        s.op("vector", lambda e: e.tensor_tensor(out=sm[:, 0:64], in0=lamv[:, 0, :], in1=lamv[:, 1, :], op=ALU.mult), reads=["lamv"], writes=["sm"])
        s.op("vector", lambda e: e.reduce_sum(out=lam[:, 0:1], in_=sm[:, 0:64], axis=AX.X), reads=["sm"], writes=["lam0"])
        s.op("vector", lambda e: e.tensor_tensor(out=sm[:, 0:64], in0=lamv[:, 2, :], in1=lamv[:, 3, :], op=ALU.mult), reads=["lamv", "lam0"], writes=["sm"])
        s.op("vector", lambda e: e.reduce_sum(out=lam[:, 1:2], in_=sm[:, 0:64], axis=AX.X), reads=["sm"], writes=["lam1"])
        s.op("scalar", lambda e: e.activation(out=lam[:, 2:4], in_=lam[:, 0:2], func=AF.Exp), reads=["lam0", "lam1"], writes=["lam2"])
        s.op("vector", lambda e: e.tensor_tensor(out=lam[:, 4:5], in0=lam[:, 3:4], in1=lam[:, 2:3], op=ALU.subtract), reads=["lam2"], writes=["lam4"])
        s.op("vector", lambda e: e.tensor_tensor(out=lam[:, 5:6], in0=lam[:, 4:5], in1=self.lami[:, 0:1], op=ALU.subtract), reads=["lam4", "lam_init"], writes=["neglam"])
        s.op("vector", lambda e: e.tensor_tensor(out=lam[:, 6:7], in0=self.dng[:, 0:1], in1=self.lami[:, 0:1], op=ALU.mult), reads=["dng", "lam_init"], writes=["lam6"])
        s.op("vector", lambda e: e.tensor_tensor(out=lam[:, 7:8], in0=self.dng[:, 0:1], in1=lam[:, 6:7], op=ALU.subtract), reads=["dng", "lam6"], writes=["gsc"])

    def diff(self, B, hh, hook=None):
        s, c, PS, F = self.s, self.c, self.PS, self.F
        neglam = self.lam[:, 5:6]
        gsc = self.lam[:, 7:8]
        items = []
        for i in range(4):
            nch = 16 * i + 16
            for ch in range(nch):
                for m in range(2):
                    items.append(dict(i=i, ch=ch, m=m, nch=nch, m_=ch - 16 * i))

        def st_qk(it):
            i, ch, m = it["i"], it["ch"], it["m"]
            qs = slice(i * 512, (i + 1) * 512)
            ks = slice(ch * 128, (ch + 1) * 128)
            ps_ = slice(64 * m, 64 * m + 64)
            sb_ = it["sb"] = self.rot("dS", 4)
            kt = B.KA if m == 0 else B.KB
            s.op("tensor", lambda e: e.matmul(PS[sb_][:], lhsT=kt[:, ks], rhs=B.QA[:, qs], start=True, stop=True),
                 reads=[B.kQA, B.kKA, B.kKB], writes=[("ps", sb_)])

        def st_exp(it):
            sb_ = it["sb"]
            pi = it["pi"] = self.rot("Pb", 8)
            P = self.Pb[pi]
            s.op("scalar", lambda e: e.activation(out=P[:], in_=PS[sb_][:], func=AF.Exp, scale=0.125),
                 reads=[("ps", sb_)], writes=[("Pb", pi)])
            if it["m_"] >= 0:
                m_ = it["m_"]
                s.op("gpsimd", lambda e: e.tensor_tensor(out=P[:], in0=P[:], in1=c["mask_le"][:, m_, :], op=ALU.mult),
                     reads=[("Pb", pi), "mask_le"], writes=[("Pb", pi)])

        def st_pv(it):
            i, ch, m, nch, pi = it["i"], it["ch"], it["m"], it["nch"], it["pi"]
            P = self.Pb[pi]
            ob_, sb2 = 4 + 2 * m, 5 + 2 * m
            s.op("tensor", lambda e: e.matmul(PS[ob_][:], lhsT=B.V[:, ch, :], rhs=P[:], start=(ch == 0), stop=(ch == nch - 1)),
                 reads=[B.kV, ("Pb", pi)], writes=[("ps", ob_)])
            s.op("tensor", lambda e: e.matmul(PS[sb2][:], lhsT=c["ones_bf"][:], rhs=P[:], start=(ch == 0), stop=(ch == nch - 1)),
                 reads=["ones_bf", ("Pb", pi)], writes=[("ps", sb2)])
            if ch == nch - 1 and m == 1:
                finalize(i)

        def finalize(i):
            qs = slice(i * 512, (i + 1) * 512)
            s.op("vector", lambda e: e.reciprocal(out=F[0][:], in_=PS[5][:]), reads=[("ps", 5)], writes=[("F", 0)])
            s.op("vector", lambda e: e.reciprocal(out=F[1][:], in_=PS[7][:]), reads=[("ps", 7)], writes=[("F", 1)])
            s.op("vector", lambda e: e.tensor_tensor(out=F[2][:], in0=PS[4][:], in1=F[0][:], op=ALU.mult), reads=[("ps", 4), ("F", 0)], writes=[("F", 2)])
            s.op("vector", lambda e: e.scalar_tensor_tensor(out=F[3][:], in0=PS[6][:], scalar=neglam, in1=F[1][:], op0=ALU.mult, op1=ALU.mult),
                 reads=[("ps", 6), ("F", 1), "neglam"], writes=[("F", 3)])
            s.op("gpsimd", lambda e: e.tensor_tensor(out=F[2][:], in0=F[2][:], in1=F[3][:], op=ALU.add), reads=[("F", 2), ("F", 3)], writes=[("F", 2)])
            sqb = self.sqb
            s.op("scalar", lambda e: e.activation(out=sqb[:], in_=F[2][:], func=AF.Square), reads=[("F", 2)], writes=["sqb"])
            nb = self.rot("dS", 4)
            s.op("tensor", lambda e: e.matmul(PS[nb][:], lhsT=c["ones_bf"][:], rhs=sqb[:], start=True, stop=True), reads=["ones_bf", "sqb"], writes=[("ps", nb)])
            s.op("scalar", lambda e: e.activation(out=F[4][:], in_=PS[nb][:], func=AF.Ln, scale=1.0 / 128, bias=self.ctx.epsc[:, 0:1]), reads=[("ps", nb), "epsc"], writes=[("F", 4)])
            s.op("scalar", lambda e: e.activation(out=F[4][:], in_=F[4][:], func=AF.Exp, scale=-0.5), reads=[("F", 4)], writes=[("F", 4)])
            oi = self.rot("ob", 2)
            ob = self.ob[oi]
            s.op("vector", lambda e: e.scalar_tensor_tensor(out=ob[:], in0=F[2][:], scalar=gsc, in1=F[4][:], op0=ALU.mult, op1=ALU.mult),
                 reads=[("F", 2), ("F", 4), "gsc"], writes=[("ob", oi)])
            s.dma("sync", self.o_dst[hh, :, qs], ob[:], reads=[("ob", oi)])

        pipeline(items, [st_qk, st_exp, st_pv], [0, 1, 3], hook)

    def sb2(self, B, hh):
        s, c, PS, F = self.s, self.c, self.PS, self.F
        items = []
        for i in range(4):
            nch = 16 * i + 16
            for ch in range(nch - 1, -1, -1):
                for h in range(2):
                    items.append(dict(i=i, ch=ch, nch=nch, m_=ch - 16 * i, h=h))

        def st_qk(it):
            i, ch = it["i"], it["ch"]
            qs = slice(i * 512, (i + 1) * 512)
            ks = slice(ch * 128, (ch + 1) * 128)
            zb = it["zb"] = (0, 1)[self.rot("sZ2", 2)]
            kt = B.KA if it["h"] == 0 else B.KB
            s.op("tensor", lambda e: e.matmul(PS[zb][:], lhsT=kt[:, ks], rhs=B.QA[:, qs], start=True, stop=True),
                 reads=[B.kQA, B.kKA, B.kKB], writes=[("ps", zb)])

        def st_exp(it):
            zb = it["zb"]
            fi = it["fi"] = self.rot("sE", 8)
            ef = self.E[fi]
            s.op("scalar", lambda e: e.activation(out=ef[:], in_=PS[zb][:], func=AF.Exp, scale=0.125),
                 reads=[("ps", zb)], writes=[("E", fi)])

        def st_mask(it):
            if it["m_"] >= 0:
                fi, m_ = it["fi"], it["m_"]
                ef = self.E[fi]
                s.op("vector", lambda e: e.tensor_tensor(out=ef[:], in0=ef[:], in1=c["mask_lt"][:, m_, :], op=ALU.mult),
                     reads=[("E", fi), "mask_lt"], writes=[("E", fi)])

        def st_ln(it):
            fi = it["fi"]
            ef = self.E[fi]
            pi = it["pi"] = self.rot("Pb", 8)
            spb = self.Pb[pi]
            s.op("scalar", lambda e: e.activation(out=spb[:], in_=ef[:], func=AF.Ln, scale=1.0, bias=self.ctx.onec[:, 0:1]),
                 reads=[("E", fi), "onec"], writes=[("Pb", pi)])

        def st_tri(it):
            pi = it["pi"]
            spb = self.Pb[pi]
            lb = it["lb"] = 2 + self.rot("sL", 2)
            cb = it["cb"] = 4 + self.rot("sC", 2)
            s.op("tensor", lambda e: e.matmul(PS[lb][:], lhsT=c["tri"][:], rhs=spb[:], start=True, stop=True),
                 reads=["tri", ("Pb", pi)], writes=[("ps", lb)])
            s.op("tensor", lambda e: e.matmul(PS[cb][:], lhsT=c["ones_bf"][:], rhs=spb[:], start=True, stop=True),
                 reads=["ones_bf", ("Pb", pi)], writes=[("ps", cb)])

        def st_w(it):
            lb, cb, fi = it["lb"], it["cb"], it["fi"]
            ef = self.E[fi]
            h = it["h"]
            cy = self.carry if h == 0 else self.carry2
            ck = ("carry", h)
            if it["ch"] == it["nch"] - 1:
                s.op("gpsimd", lambda e: e.memset(cy[:], 0.0), writes=[ck])
            wi = it["wi"] = self.rot("sW", 4)
            wf = F[wi]
            s.op("vector", lambda e: e.tensor_tensor(out=wf[:], in0=PS[lb][:], in1=cy[:], op=ALU.add),
                 reads=[("ps", lb), ck], writes=[("F", wi)])
            if it["ch"] > 0:
                s.op("vector", lambda e: e.tensor_tensor(out=cy[:], in0=cy[:], in1=PS[cb][:], op=ALU.add),
                     reads=[ck, ("ps", cb)], writes=[ck])

        def st_g(it):
            wi = it["wi"]
            wf = F[wi]
            s.op("scalar", lambda e: e.activation(out=wf[:], in_=wf[:], func=AF.Exp, scale=-1.0), reads=[("F", wi)], writes=[("F", wi)])

        def st_a(it):
            fi, wi = it["fi"], it["wi"]
            ef, wf = self.E[fi], F[wi]
            ai = it["ai"] = self.rot("Pb", 8)
            A = self.Pb[ai]
            a_eng = "vector" if (it["ch"] % 6 == 0) else "gpsimd"
            s.op(a_eng, lambda e: e.tensor_tensor(out=A[:], in0=ef[:], in1=wf[:], op=ALU.mult), reads=[("E", fi), ("F", wi)], writes=[("Pb", ai)])

        def st_pv(it):
            i, ch, nch, ai, h = it["i"], it["ch"], it["nch"], it["ai"], it["h"]
            A = self.Pb[ai]
            ob_ = 6 + h
            hp = slice(64 * h, 64 * h + 64)
            s.op("tensor", lambda e: e.matmul(PS[ob_][:, :], lhsT=B.V[:, ch, :], rhs=A[:], start=(ch == nch - 1), stop=(ch == 0)),
                 reads=[B.kV, ("Pb", ai)], writes=[("ps", ob_)])
            if ch == 0:
                qs = slice(i * 512, (i + 1) * 512)
                oi = self.rot("ob", 2)
                ob = self.ob[oi]
                s.op("scalar", lambda e: e.activation(out=ob[hp, :], in_=PS[ob_][hp, :], func=AF.Copy), reads=[("ps", ob_)], writes=[("ob", oi)])
                s.dma("sync", self.o_dst[4 + hh, hp, qs], ob[hp, :], reads=[("ob", oi)])

        pipeline(items, [st_qk, st_exp, st_mask, st_ln, st_tri, st_w, st_g, st_a, st_pv], [0, 1, 2, 3, 4, 5, 6, 7, 8])


    def sb(self, B, hh, h, hook=None):
        s, c, PS, F = self.s, self.c, self.PS, self.F
        hp = slice(64 * h, 64 * h + 64)
        items = []
        for i in range(4):
            nch = 16 * i + 16
            for ch in range(nch - 1, -1, -1):
                items.append(dict(i=i, ch=ch, nch=nch, m_=ch - 16 * i))

        def st_qk(it):
            i, ch = it["i"], it["ch"]
            qs = slice(i * 512, (i + 1) * 512)
            ks = slice(ch * 128, (ch + 1) * 128)
            zb = it["zb"] = (0, 1, 7)[self.rot("sZ", 3)] if hook is None else (0, 1)[self.rot("sZ2", 2)]
            kt = B.KA if h == 0 else B.KB
            s.op("tensor", lambda e: e.matmul(PS[zb][:], lhsT=kt[:, ks], rhs=B.QA[:, qs], start=True, stop=True),
                 reads=[B.kQA, B.kKA, B.kKB], writes=[("ps", zb)])

        def st_exp(it):
            zb = it["zb"]
            fi = it["fi"] = self.rot("sE", 8)
            ef = self.E[fi]
            s.op("scalar", lambda e: e.activation(out=ef[:], in_=PS[zb][:], func=AF.Exp, scale=0.125),
                 reads=[("ps", zb)], writes=[("E", fi)])

        def st_mask(it):
            if it["m_"] >= 0:
                fi, m_ = it["fi"], it["m_"]
                ef = self.E[fi]
                s.op("vector", lambda e: e.tensor_tensor(out=ef[:], in0=ef[:], in1=c["mask_lt"][:, m_, :], op=ALU.mult),
                     reads=[("E", fi), "mask_lt"], writes=[("E", fi)])

        def st_ln(it):
            fi = it["fi"]
            ef = self.E[fi]
            pi = it["pi"] = self.rot("Pb", 8)
            spb = self.Pb[pi]
            s.op("scalar", lambda e: e.activation(out=spb[:], in_=ef[:], func=AF.Ln, scale=1.0, bias=self.ctx.onec[:, 0:1]),
                 reads=[("E", fi), "onec"], writes=[("Pb", pi)])

        def st_tri(it):
            pi = it["pi"]
            spb = self.Pb[pi]
            lb = it["lb"] = 2 + self.rot("sL", 2)
            cb = it["cb"] = 4 + self.rot("sC", 2)
            s.op("tensor", lambda e: e.matmul(PS[lb][:], lhsT=c["tri"][:], rhs=spb[:], start=True, stop=True),
                 reads=["tri", ("Pb", pi)], writes=[("ps", lb)])
            s.op("tensor", lambda e: e.matmul(PS[cb][:], lhsT=c["ones_bf"][:], rhs=spb[:], start=True, stop=True),
                 reads=["ones_bf", ("Pb", pi)], writes=[("ps", cb)])

        def st_w(it):
            lb, cb, fi = it["lb"], it["cb"], it["fi"]
            ef = self.E[fi]
            if it["ch"] == it["nch"] - 1:
                s.op("gpsimd", lambda e: e.memset(self.carry[:], 0.0), writes=["carry"])
            wi = it["wi"] = self.rot("sW", 4)
            wf = F[wi]
            s.op("vector", lambda e: e.tensor_tensor(out=wf[:], in0=PS[lb][:], in1=self.carry[:], op=ALU.add),
                 reads=[("ps", lb), "carry"], writes=[("F", wi)])
            if it["ch"] > 0:
                s.op("vector", lambda e: e.tensor_tensor(out=self.carry[:], in0=self.carry[:], in1=PS[cb][:], op=ALU.add),
                     reads=["carry", ("ps", cb)], writes=["carry"])

        def st_g(it):
            wi = it["wi"]
            wf = F[wi]
            s.op("scalar", lambda e: e.activation(out=wf[:], in_=wf[:], func=AF.Exp, scale=-1.0), reads=[("F", wi)], writes=[("F", wi)])

        def st_a(it):
            fi, wi = it["fi"], it["wi"]
            ef, wf = self.E[fi], F[wi]
            ai = it["ai"] = self.rot("Pb", 8)
            A = self.Pb[ai]
            a_eng = "vector" if (it["ch"] % 6 == 0) else "gpsimd"
            s.op(a_eng, lambda e: e.tensor_tensor(out=A[:], in0=ef[:], in1=wf[:], op=ALU.mult), reads=[("E", fi), ("F", wi)], writes=[("Pb", ai)])

        def st_pv(it):
            i, ch, nch, ai = it["i"], it["ch"], it["nch"], it["ai"]
            A = self.Pb[ai]
            s.op("tensor", lambda e: e.matmul(PS[6][:, :], lhsT=B.V[:, ch, :], rhs=A[:], start=(ch == nch - 1), stop=(ch == 0)),
                 reads=[B.kV, ("Pb", ai)], writes=[("ps", 6)])
            if ch == 0:
                qs = slice(i * 512, (i + 1) * 512)
                oi = self.rot("ob", 2)
                ob = self.ob[oi]
                s.op("scalar", lambda e: e.activation(out=ob[hp, :], in_=PS[6][hp, :], func=AF.Copy), reads=[("ps", 6)], writes=[("ob", oi)])
                s.dma("sync", self.o_dst[4 + hh, hp, qs], ob[hp, :], reads=[("ob", oi)])

        pipeline(items, [st_qk, st_exp, st_mask, st_ln, st_tri, st_w, st_g, st_a, st_pv], [0, 1, 2, 3, 4, 5, 6, 7, 8], hook)

    def moba_pre(self, B, hh, h):
        s, c, PS, F = self.s, self.c, self.PS, self.F
        hp = slice(64 * h, 64 * h + 64)
        hi = slice(64, 128)
        km, kmb = self.km, self.kmb
        s.op("vector", lambda e: e.tensor_reduce(out=km[hi, :], in_=B.KA[hi, :].rearrange("p (n k) -> p n k", k=256), axis=AX.X, op=ALU.add),
             reads=[B.kKA], writes=["km"])
        s.op("vector", lambda e: e.tensor_scalar(out=kmb[hi, :], in0=km[hi, :], scalar1=1.0 / 256, scalar2=None, op0=ALU.mult), reads=["km"], writes=["kmb"])
        VA = B.KB[:, :].rearrange("p (c d) -> p c d", d=128)
        s.op("gpsimd", lambda e: e.memset(VA[:, :, 64:128], 1.0), writes=[B.kKB])
        s.op("gpsimd", lambda e: e.tensor_copy(out=VA[:, :, 0:64], in_=B.V[:, :, hp]), reads=[B.kV], writes=[B.kKB])
        gm, bt, mx = self.gm, self.btall, self.mxall
        gbank = 7

        def gates(e):
            ins = None
            for qi in range(16):
                ins = e.matmul(PS[gbank][:, qi * 32:(qi + 1) * 32], lhsT=B.QA[hi, qi * 128:(qi + 1) * 128], rhs=kmb[hi, :], start=True, stop=True)
            return ins
        s.op("tensor", gates, reads=[B.kQA, "kmb"], writes=[("ps", gbank)])
        s.op("vector", lambda e: e.tensor_tensor(out=gm[:], in0=PS[gbank][:], in1=c["mvneg"][:].rearrange("p a b -> p (a b)"), op=ALU.add),
             reads=[("ps", gbank), "mvneg"], writes=["gm"])

        def maxes(e):
            ins = None
            for qi in range(16):
                ins = e.max(out=mx[:, qi, :], in_=gm[:, qi * 32:(qi + 1) * 32])
            return ins
        s.op("vector", maxes, reads=["gm"], writes=["mxall"])

        def sels(e):
            ins = None
            for qi in range(16):
                ins = e.tensor_scalar(out=bt[:, qi * 32:(qi + 1) * 32], in0=gm[:, qi * 32:(qi + 1) * 32], scalar1=mx[:, qi, 2:3], scalar2=None, op0=ALU.is_ge)
            return ins
        s.op("vector", sels, reads=["gm", "mxall"], writes=["btall"])
        s.op("vector", lambda e: e.tensor_tensor(out=bt[:], in0=bt[:], in1=c["mvalid"][:].rearrange("p a b -> p (a b)"), op=ALU.mult), reads=["btall", "mvalid"], writes=["btall"])
        s.op("vector", lambda e: e.tensor_tensor(out=bt[:], in0=bt[:], in1=c["mown"][:].rearrange("p a b -> p (a b)"), op=ALU.add), reads=["btall", "mown"], writes=["btall"])
        s.op("vector", lambda e: e.tensor_scalar(out=bt[:], in0=bt[:], scalar1=BIG, scalar2=-BIG, op0=ALU.mult, op1=ALU.add), reads=["btall"], writes=["btall"])
        for g4 in range(4):
            tb = 7

            def tr(e, g4=g4, tb=tb):
                ins = None
                for a in range(4):
                    qi = 4 * g4 + a
                    ins = e.transpose(PS[tb][0:32, a * 128:(a + 1) * 128], bt[:, qi * 32:(qi + 1) * 32], c["ident"][:])
                return ins
            s.op("tensor", tr, reads=["btall", "ident"], writes=[("ps", tb)])
            s.op("vector", lambda e, g4=g4, tb=tb: e.tensor_copy(out=B.QA[0:32, g4 * 512:(g4 + 1) * 512], in_=PS[tb][0:32, :]), reads=[("ps", tb)],
                 writes=[("QAb", B.si, 4 * g4 + a) for a in range(4)])

    def moba(self, B, hh, h, hook=None):
        s, c, PS, F = self.s, self.c, self.PS, self.F
        hp = slice(64 * h, 64 * h + 64)
        VA = B.KB[:, :].rearrange("p (c d) -> p c d", d=128)
        items = []
        for i in range(4):
            nch = 16 * i + 16
            for ch in range(nch):
                items.append(dict(i=i, ch=ch, nch=nch, m_=ch - 16 * i))

        def st_qk(it):
            i, ch = it["i"], it["ch"]
            qs = slice(i * 512, (i + 1) * 512)
            ks = slice(ch * 128, (ch + 1) * 128)
            sb_ = it["sb"] = self.rot("mS", 4)
            s.op("tensor", lambda e: e.matmul(PS[sb_][:], lhsT=B.KA[:, ks], rhs=B.QA[:, qs], start=True, stop=True),
                 reads=[B.kQA, B.kKA] + [("QAb", B.si, 4 * i + a) for a in range(4)], writes=[("ps", sb_)])

        def st_exp(it):
            sb_ = it["sb"]
            pi = it["pi"] = self.rot("Pb", 8)
            P = self.Pb[pi]
            s.op("scalar", lambda e: e.activation(out=P[:], in_=PS[sb_][:], func=AF.Exp, scale=0.125),
                 reads=[("ps", sb_)], writes=[("Pb", pi)])
            if it["m_"] >= 0:
                m_ = it["m_"]
                s.op("gpsimd", lambda e: e.tensor_tensor(out=P[:], in0=P[:], in1=c["mask_le"][:, m_, :], op=ALU.mult),
                     reads=[("Pb", pi), "mask_le"], writes=[("Pb", pi)])

        def st_pv(it):
            i, ch, nch, pi = it["i"], it["ch"], it["nch"], it["pi"]
            P = self.Pb[pi]
            ob_ = 4 + (i % 2)
            s.op("tensor", lambda e: e.matmul(PS[ob_][:], lhsT=VA[:, ch, :], rhs=P[:], start=(ch == 0), stop=(ch == nch - 1)),
                 reads=[B.kKB, ("Pb", pi)], writes=[("ps", ob_)])
            if ch == nch - 1:
                qs = slice(i * 512, (i + 1) * 512)
                ri = self.rot("mR", 2)
                s.op("vector", lambda e: e.reciprocal(out=F[ri][64:128, :], in_=PS[ob_][64:128, :]), reads=[("ps", ob_)], writes=[("F", ri)])
                s.dma("sync", F[2 + ri][0:64, :], F[ri][64:128, :], reads=[("F", ri)], writes=[("F", 2 + ri)])
                oi = self.rot("ob", 2)
                ob = self.ob[oi]
                s.op("vector", lambda e: e.tensor_tensor(out=ob[0:64, :], in0=PS[ob_][0:64, :], in1=F[2 + ri][0:64, :], op=ALU.mult),
                     reads=[("ps", ob_), ("F", 2 + ri)], writes=[("ob", oi)])
                s.dma("sync", self.o_dst[8 + hh, hp, qs], ob[0:64, :], reads=[("ob", oi)])

        pipeline(items, [st_qk, st_exp, st_pv], [0, 1, 3], hook)


def build_fused(n_layers=2):
    nc = bass.Bass("TRN2", target_bir_lowering=False)
    nc.allow_low_precision("bf16 matmul operands, fp32 accumulation")
    s = Sched(nc)
    ctx = Ctx()
    ctx.nc, ctx.s, ctx.cnt = nc, s, {}
    dt = nc.dram_tensor

    def din(name, shape, dty=F32):
        return dt(name, list(shape), dty, kind="ExternalInput").ap()
    x_in = din("x_in", [128, NC8, TOK])
    x_out = dt("x_out", [128, NC8, TOK], F32, kind="ExternalOutput").ap()
    ones_d = din("ones_bf", [128, 128], BF16)
    ctx.blk1h_d = din("blk1h", [32, S], BF16)
    ctx.acon_d = {}
    for name, shape, dty in (("mask_le", [128, 16, 512], BF16), ("mask_lt", [128, 16, 512], BF16), ("tri", [128, 128], BF16),
                             ("ones_bf", [128, 128], BF16), ("ident", [128, 128], F32),
                             ("mvneg", [128, 16, 32], F32), ("mvalid", [128, 16, 32], F32), ("mown", [128, 16, 32], F32)):
        d = ones_d if name == "ones_bf" else din(name, shape, dty)
        ctx.acon_d[name] = (d, shape, dty)
    cos_d = din("rope_cos", [128, TOK])
    sin_d = din("rope_sin", [128, TOK])
    pm_d = din("rope_perm", [128, 128], BF16)
    LW = []
    for l in range(n_layers):
        p = "l%d_" % l
        W = {}
        for f in ("ffn1", "ffn2"):
            W[f] = (din(p + f + "_wg", [NF, 128, NC8, 128]), din(p + f + "_wu", [NF, 128, NC8, 128]), din(p + f + "_wd", [NC8, 128, NF, 128]))
        for g in ("g_ffn1_pre", "g_ffn1_post", "g_mix_pre", "g_mix_post", "g_ffn2_pre", "g_ffn2_post"):
            W[g] = din(p + g, [128, NC8])
        W["w_fm"] = din(p + "w_in_fm", [48, 128, NC8, 128])
        W["w_v"] = din(p + "w_in_v", [3, 128, NC8, 512])
        W["w_mo"] = din(p + "w_mo", [3, NC8, 128, 4, 128])
        W["w_out"] = din(p + "w_out_t", [NC8, 128, NC8, 128])
        W["lamv"] = din(p + "lamv", [128, 4, 64])
        W["lam_init"] = din(p + "lam_init", [128, 1])
        W["dng"] = din(p + "dng", [128, 1])
        W["q_loc"] = dt(p + "q_loc", [12, 128, TOK], BF16).ap()
        W["g_scr"] = dt(p + "g_scr", [24, 128, TOK], F32).ap()
        W["o_loc"] = dt(p + "o_loc", [12, 128, TOK], BF16).ap()
        W["send"] = [dt(p + "send%d" % b, [256, TOK], BF16) for b in range(SEND_ROWS // 256)]
        W["gath"] = [dt(p + "gath%d" % b, [1024, TOK], BF16) for b in range(SEND_ROWS // 256)]
        LW.append(W)
    xbuf = dt("xbuf", [128, NC8, TOK], F32).ap()
    ctx.PS = [s.ps("ps%d" % i, [128, 512]) for i in range(8)]
    ctx.ones = s.sb("ones", [128, 128], BF16)
    s.dma("sync", ctx.ones[:], ones_d, writes=["ones"])
    ctx.epsc = s.sb("epsc", [128, 1], F32)
    s.op("gpsimd", lambda e: e.memset(ctx.epsc[:], EPS), writes=["epsc"])
    ctx.onec = s.sb("onec", [128, 1], F32)
    s.op("gpsimd", lambda e: e.memset(ctx.onec[:], 1.0), writes=["onec"])
    s.barrier()
    final = []

    def reinit_consts():
        pass

    def proj_P(W):
        return dict(w_fm=W["w_fm"], w_v=W["w_v"], cos=cos_d, sin=sin_d, pm=pm_d, q_dst=W["q_loc"], send=[t.ap() for t in W["send"]], g_dst=W["g_scr"])

    def merge_M(W):
        return dict(a_src=W["o_loc"], g_src=W["g_scr"], w_mo=W["w_mo"], w_out=W["w_out"])

    for l in range(n_layers + 1):
        s.phase_begin()
        ts = TokStage(ctx)
        merge = None
        ffns = []
        proj = None
        if l > 0:
            Wp = LW[l - 1]
            ts.load_gain("mixpost", Wp["g_mix_post"])
            ts.load_gain("f2pre", Wp["g_ffn2_pre"])
            ts.load_gain("f2post", Wp["g_ffn2_post"], half=True)
            merge = (merge_M(Wp), "mixpost")
            ffns.append((Wp["ffn2"], "f2pre", "f2post"))
        if l < n_layers:
            W = LW[l]
            ts.load_gain("f1pre", W["g_ffn1_pre"])
            ts.load_gain("f1post", W["g_ffn1_post"], half=True)
            ts.load_gain("mixpre", W["g_mix_pre"])
            ffns.append((W["ffn1"], "f1pre", "f1post"))
            proj = (proj_P(W), "mixpre")
        last = (l == n_layers)
        ts.run(x_in if l == 0 else xbuf, x_out if last else xbuf, merge=merge, ffns=ffns, proj=proj, final=final if last else None)
        if last:
            s.join("sync", final)
            break
        s.phase_end()
        W = LW[l]
        for b in range(12):
            s.collective(lambda e, W=W, b=b: e.collective_compute("AllGather", ALU.bypass, replica_groups=[[0, 1, 2, 3], [4, 5, 6, 7]],
                                                                   ins=[W["send"][b].ap().opt()], outs=[W["gath"][b].ap().opt()]),
                         writes=[("gath", b)])
        s.phase_begin()
        AttnStage(ctx, W["q_loc"], [t.ap() for t in W["gath"]], W["o_loc"], (W["lamv"], W["lam_init"], W["dng"]))
        s.phase_end()
    s.emit()
    s.close()
    return nc, s


BF = ml_dtypes.bfloat16
NCORES = 8


def fm_in(w, nk, no):
    return np.ascontiguousarray(w.reshape(nk, 128, no, 128).transpose(2, 1, 0, 3))


def gain_t(g):
    return np.ascontiguousarray(g.reshape(8, 128).T)


def core_positions(c):
    t = c % 4
    return np.concatenate([np.arange((4 * i + t) * 512, (4 * i + t + 1) * 512) for i in range(4)])


def x_to_cores(x):
    out = []
    for c in range(NCORES):
        xc = x[c // 4][core_positions(c)]
        out.append(np.ascontiguousarray(xc.T.reshape(8, 128, 2048).transpose(1, 0, 2)))
    return out


def x_from_cores(xs):
    out = np.empty((2, 8192, 1024), np.float32)
    for c in range(NCORES):
        out[c // 4][core_positions(c)] = xs[c].transpose(1, 0, 2).reshape(1024, 2048).T
    return out


def rope_consts():
    inv_freq = (np.float32(500000.0) ** (-np.arange(0, 16, 2, dtype=np.float32) / np.float32(16))).astype(np.float32)
    pos = np.arange(8192, dtype=np.float32)
    ang = pos[:, None] * inv_freq[None, :]
    c = np.cos(ang).astype(np.float32)
    s = np.sin(ang).astype(np.float32)
    cosT = np.ones((128, 8192), np.float32)
    sinT = np.zeros((128, 8192), np.float32)
    pm = np.zeros((128, 128), np.float32)
    for p in range(128):
        r = p % 64
        if r < 8:
            cosT[p] = c[:, r]
            sinT[p] = -s[:, r]
            pm[p + 8, p] = 1.0
        elif r < 16:
            cosT[p] = c[:, r - 8]
            sinT[p] = s[:, r - 8]
            pm[p - 8, p] = 1.0
    return cosT, sinT, pm.astype(BF)


def core_consts(c, cosT, sinT):
    t = c % 4
    pos = core_positions(c)
    k = np.arange(128)[:, None, None]
    j = np.arange(4)[None, :, None]
    q = np.arange(512)[None, None, :]
    dle = ((128 * j + k) <= q).astype(np.float32)
    dlt = ((128 * j + k) < q).astype(np.float32)
    mle = np.zeros((128, 16, 512), np.float32)
    mlt = np.zeros((128, 16, 512), np.float32)
    for u in range(4):
        if u < t:
            mle[:, 4 * u:4 * u + 4] = 1.0
            mlt[:, 4 * u:4 * u + 4] = 1.0
        elif u == t:
            mle[:, 4 * u:4 * u + 4] = dle
            mlt[:, 4 * u:4 * u + 4] = dlt
    mvneg = np.zeros((16, 32), np.float32)
    mvalid = np.zeros((16, 32), np.float32)
    mown = np.zeros((16, 32), np.float32)
    for qi in range(16):
        i, jj = qi // 4, qi % 4
        own = ((4 * i + t) * 512 + jj * 128) // 256
        mvalid[qi, :own] = 1.0
        mvneg[qi, own:] = -30000.0
        mown[qi, own] = 1.0
    bc = lambda a: np.ascontiguousarray(np.broadcast_to(a[None], (128,) + a.shape))
    return {"rope_cos": np.ascontiguousarray(cosT[:, pos]), "rope_sin": np.ascontiguousarray(sinT[:, pos]),
            "mask_le": mle.astype(BF), "mask_lt": mlt.astype(BF), "mvneg": bc(mvneg), "mvalid": bc(mvalid), "mown": bc(mown)}


def shared_consts(pm):
    tri = (np.arange(128)[:, None] >= np.arange(128)[None, :]).astype(BF)
    blk = (np.arange(8192)[None, :] // 256 == np.arange(32)[:, None]).astype(BF)
    return {"tri": tri, "ones_bf": np.ones((128, 128), BF), "ident": np.eye(128, dtype=np.float32), "blk1h": blk, "rope_perm": pm}


_PROG = {}


def _proj_w(w_in):
    cols = [(0, 512), (1536, 2048), (3072, 3584), (512, 1024), (2048, 2560), (3584, 4096), (4608, 7680)]
    wq = np.concatenate([w_in[:, a:b] for a, b in cols], axis=1)
    wv = np.stack([np.ascontiguousarray(w_in[:, a:a + 512].reshape(8, 128, 512).transpose(1, 0, 2)) for a in (1024, 2560, 4096)])
    return fm_in(wq, 8, 48), wv


def kernel(x, w_in, w_diff_o, w_sb_o, w_moba_o, w_out, lam_q1, lam_k1, lam_q2, lam_k2,
           diff_norm_g, ffn1_wg, ffn1_wu, ffn1_wd, ffn2_wg, ffn2_wu, ffn2_wd,
           g_ffn1_pre, g_ffn1_post, g_mix_pre, g_mix_post, g_ffn2_pre, g_ffn2_post):
    f32 = lambda a: np.ascontiguousarray(np.asarray(a, dtype=np.float32))
    L = 2
    if "nc" not in _PROG:
        _PROG["nc"] = build_fused(L)[0]
    nc = _PROG["nc"]
    cosT, sinT, pm = rope_consts()
    com = shared_consts(pm)
    for l in range(L):
        p = "l%d_" % l
        for f, (wg, wu, wd) in (("ffn1", (ffn1_wg, ffn1_wu, ffn1_wd)), ("ffn2", (ffn2_wg, ffn2_wu, ffn2_wd))):
            com[p + f + "_wg"] = fm_in(f32(wg[l]), 8, 22)
            com[p + f + "_wu"] = fm_in(f32(wu[l]), 8, 22)
            com[p + f + "_wd"] = fm_in(f32(wd[l]), 22, 8)
        for name, g in (("g_ffn1_pre", g_ffn1_pre), ("g_ffn1_post", g_ffn1_post), ("g_mix_pre", g_mix_pre),
                        ("g_mix_post", g_mix_post), ("g_ffn2_pre", g_ffn2_pre), ("g_ffn2_post", g_ffn2_post)):
            com[p + name] = gain_t(f32(g[l]))
        com[p + "w_in_fm"], com[p + "w_in_v"] = _proj_w(f32(w_in[l]))
        com[p + "w_mo"] = np.stack([fm_in(f32(w_diff_o[l]), 4, 8), fm_in(f32(w_sb_o[l]), 4, 8), fm_in(f32(w_moba_o[l]), 4, 8)])
        com[p + "w_out_t"] = fm_in(f32(w_out[l]), 8, 8)
        lamv = np.stack([f32(lam_q1[l]), f32(lam_k1[l]), f32(lam_q2[l]), f32(lam_k2[l])])
        com[p + "lamv"] = np.ascontiguousarray(np.broadcast_to(lamv[None], (128, 4, 64)))
        com[p + "lam_init"] = np.full((128, 1), 0.8 - 0.6 * math.exp(-0.3 * l), np.float32)
        com[p + "dng"] = np.ascontiguousarray(f32(diff_norm_g[l]).reshape(128, 1))
    xs = x_to_cores(f32(x))
    in_maps = [dict(com, x_in=xs[c], **core_consts(c, cosT, sinT)) for c in range(NCORES)]
    res = run_bass_kernel_spmd(nc, in_maps, core_ids=list(range(NCORES)))
    return x_from_cores([res.results[c]["x_out"] for c in range(NCORES)])
```

```python
import math
import contextlib
import numpy as np
import ml_dtypes
import concourse.bass as bass
import concourse.mybir as mybir
from concourse.bass_utils import run_bass_kernel_spmd

F32 = mybir.dt.float32
BF16 = mybir.dt.bfloat16
I32 = mybir.dt.int32
AF = mybir.ActivationFunctionType
ALU = mybir.AluOpType
AX = mybir.AxisListType

ENGS = ["tensor", "vector", "scalar", "gpsimd", "sync"]


class Op:
    __slots__ = ("eng", "fn", "dma", "deps", "idx", "ticket", "needs_inc", "dsem", "dval", "dprev", "cc")

    def __init__(self, eng, fn, dma, idx):
        self.eng = eng
        self.fn = fn
        self.dma = dma
        self.deps = ()
        self.idx = idx
        self.ticket = None
        self.needs_inc = False
        self.dsem = None
        self.dval = None
        self.dprev = None
        self.cc = None


class Sched:
    def __init__(self, nc, n_dma_sems=28):
        self.nc = nc
        self.ops = []
        self.last_w = {}
        self.readers = {}
        self.n_dma_sems = n_dma_sems
        self.stack = contextlib.ExitStack()
        self.pstack = None
        self.n_dma = 0
        self.n_dma_by = {}
        self.n_cc = 0
        self.since_barrier = []

    def sb(self, name, shape, dtype):
        st = self.pstack if self.pstack is not None else self.stack
        if self.pstack is not None:
            name = "%s_ph%d" % (name, self.n_phase)
        return st.enter_context(self.nc.sbuf_tensor(name, list(shape), dtype))

    def phase_begin(self):
        self.n_phase = getattr(self, "n_phase", 0) + 1
        self.pstack = contextlib.ExitStack()

    def phase_end(self):
        self.barrier()
        self.pstack.close()
        self.pstack = None

    def barrier(self):
        prev = list(self.since_barrier)
        self.since_barrier = []
        last = {}
        deps = []
        for o in prev:
            if o.dma:
                deps.append(o)
            elif o.fn is not None:
                last[o.eng] = o
        deps += list(last.values())
        for e in ENGS:
            j = Op(e, None, False, len(self.ops))
            j.deps = list(deps)
            self.ops.append(j)
        self.last_w = {}
        self.readers = {}

    def collective(self, fn, writes=(), deps=()):
        o = Op("gpsimd", fn, True, len(self.ops))
        o.deps = list(deps)
        for k in writes:
            self.last_w[k] = o
            self.readers[k] = []
        o.cc = self.n_cc
        self.n_cc += 1
        self.ops.append(o)
        self.since_barrier.append(o)
        return o

    def ps(self, name, shape, dtype=F32):
        return self.stack.enter_context(self.nc.psum_tensor(name, list(shape), dtype))

    def op(self, eng, fn, reads=(), writes=(), dma=False):
        o = Op(eng, fn, dma, len(self.ops))
        deps = {}
        for k in reads:
            w = self.last_w.get(k)
            if w is not None:
                deps[w.idx] = w
            if isinstance(k, tuple) and k and k[0] == "ps" or (isinstance(k, str) and k.startswith("ps")):
                for r in self.readers.get(k, ()):
                    if r.eng != eng:
                        deps[r.idx] = r
        for k in writes:
            w = self.last_w.get(k)
            if w is not None:
                deps[w.idx] = w
            for r in self.readers.get(k, ()):
                deps[r.idx] = r
        deps.pop(o.idx, None)
        o.deps = list(deps.values())
        for k in writes:
            self.last_w[k] = o
            self.readers[k] = []
        for k in reads:
            self.readers.setdefault(k, []).append(o)
        if dma:
            n = self.n_dma_by.get(eng, 0)
            o.dsem = (eng, n % self.n_dma_sems)
            o.dval = 16 * (n // self.n_dma_sems + 1)
            self.n_dma_by[eng] = n + 1
            self.n_dma += 1
        self.ops.append(o)
        self.since_barrier.append(o)
        return o

    def dma(self, eng, out, in_, reads=(), writes=(), **kw):
        return self.op(eng, lambda e: e.dma_start(out=out, in_=in_, **kw), reads, writes, dma=True)

    def join(self, eng, ops):
        o = Op(eng, None, False, len(self.ops))
        o.deps = list(ops)
        self.ops.append(o)
        return o

    def emit(self):
        nc = self.nc
        for o in self.ops:
            for d in o.deps:
                if not d.dma and d.fn is not None:
                    if d.eng == o.eng and o.eng == "tensor" and not o.dma:
                        continue
                    d.needs_inc = True
        cnt = {e: 0 for e in ENGS}
        for o in self.ops:
            if (not o.dma) and o.needs_inc:
                cnt[o.eng] += 1
                o.ticket = cnt[o.eng]
        esem = {e: self.stack.enter_context(nc.semaphore("es_" + e)) for e in ENGS}
        dsem = {}
        for en in self.n_dma_by:
            for i in range(min(self.n_dma_sems, self.n_dma_by[en])):
                dsem[(en, i)] = self.stack.enter_context(nc.semaphore("ds_%s_%d" % (en, i)))
        csem = [self.stack.enter_context(nc.semaphore("cs_%d" % i)) for i in range(self.n_cc)]
        by_eng = {e: [o for o in self.ops if o.eng == e] for e in ENGS}
        self.stats = {e: len(by_eng[e]) for e in ENGS}

        def run(engname, e):
            waited = {}

            def wait(key, sem, val):
                if waited.get(key, 0) >= val:
                    return
                waited[key] = val
                e.wait_ge(sem, val)

            for o in by_eng[engname]:
                for d in o.deps:
                    if d.cc is not None:
                        wait(("c", d.cc), csem[d.cc], 1)
                    elif d.dma:
                        wait(("d", d.dsem), dsem[d.dsem], d.dval)
                    else:
                        if d.eng == engname and engname == "tensor" and not o.dma:
                            continue
                        wait(("e", d.eng), esem[d.eng], d.ticket)
                if o.dma and o.cc is None and o.dval > 16:
                    wait(("d", o.dsem), dsem[o.dsem], o.dval - 16)
                if o.fn is None:
                    continue
                ins = o.fn(e)
                if o.cc is not None:
                    ins.then_inc(csem[o.cc])
                elif o.dma:
                    ins.then_inc(dsem[o.dsem], 16)
                elif o.needs_inc:
                    ins.then_inc(esem[engname], 1)

        with nc.Block() as block:
            @block.tensor
            def _(e):
                run("tensor", e)

            @block.vector
            def _(e):
                run("vector", e)

            @block.scalar
            def _(e):
                run("scalar", e)

            @block.gpsimd
            def _(e):
                run("gpsimd", e)

            @block.sync
            def _(e):
                run("sync", e)

    def close(self):
        if self.pstack is not None:
            self.pstack.close()
            self.pstack = None
        self.stack.close()


D = 1024
NC8 = 8
NF = 22
TOK = 2048
EPS = 1e-6
S = 8192
NKC = S // 128
BIG = 30000.0
SEND_ROWS = 3072


class Ctx:
    pass


def rot(ctx, name, n):
    i = ctx.cnt.get(name, 0)
    ctx.cnt[name] = i + 1
    return i % n


def pipeline(items, stages, lags, hook=None):
    n = len(items)
    for k in range(n + max(lags)):
        if hook is not None and k == n // 2:
            hook()
        for st, lag in zip(stages, lags):
            i = k - lag
            if 0 <= i < n:
                st(items[i])


class TokStage:
    def __init__(self, ctx, TG=1024):
        self.ctx = ctx
        self.nc, self.s, self.PS = ctx.nc, ctx.s, ctx.PS
        s = self.s
        self.TG = TG
        self.NT = TG // 512
        self.ones = ctx.ones
        self.epsc = ctx.epsc
        self.xg = s.sb("xg", [128, NC8, TG], F32)
        self.hb = s.sb("hb", [128, NC8, TG], BF16)
        self.act = s.sb("act", [128, NF, TG], BF16)
        self.yt = s.sb("yt", [128, NC8, TG], F32)
        self.rstd = s.sb("rstd", [128, TG], F32)
        self.sq = [s.sb("sq%d" % i, [128, 512], BF16) for i in range(2)]
        self.sg = [s.sb("sg%d" % i, [128, 512], F32) for i in range(2)]
        self.wA = [s.sb("wA%d" % i, [128, NC8, 128], BF16) for i in range(8)]
        self.wD = [s.sb("wD%d" % i, [128, NF, 128], BF16) for i in range(3)]
        self.gains = {}
        self.proj_init = False
        self.merge_init = False

    def rot(self, name, n):
        return rot(self.ctx, name, n)

    def load_gain(self, name, dram, half=False):
        s = self.s
        t = s.sb("g_" + name, [128, NC8], F32)
        s.dma("sync", t[:], dram, writes=["g_" + name])
        if half:
            s.op("vector", lambda e: e.tensor_scalar(out=t[:], in0=t[:], scalar1=0.5, scalar2=None, op0=ALU.mult),
                 reads=["g_" + name], writes=["g_" + name])
        self.gains[name] = t

    def rms_stats(self, src, srckey):
        s = self.s
        for nt in range(self.NT):
            sl = slice(nt * 512, (nt + 1) * 512)
            bank = 6
            ps = self.PS[bank]
            for c in range(NC8):
                qi = self.rot("sq", 2)
                sq = self.sq[qi]
                s.op("scalar", lambda e, sq=sq, c=c, sl=sl: e.activation(out=sq[:], in_=src[:, c, sl], func=AF.Square),
                     reads=[(srckey, c, nt)], writes=[("sq", qi)])
                s.op("tensor", lambda e, sq=sq, c=c, ps=ps: e.matmul(ps[:], lhsT=self.ones[:], rhs=sq[:], start=(c == 0), stop=(c == NC8 - 1)),
                     reads=["ones", ("sq", qi)], writes=[("ps", bank)])
            rs = self.rstd
            s.op("scalar", lambda e, ps=ps, sl=sl: e.activation(out=rs[:, sl], in_=ps[:], func=AF.Ln, bias=self.epsc[:, 0:1], scale=1.0 / D),
                 reads=[("ps", bank), "epsc"], writes=[("rstd", nt)])
            s.op("scalar", lambda e, sl=sl: e.activation(out=rs[:, sl], in_=rs[:, sl], func=AF.Exp, scale=-0.5), reads=[("rstd", nt)], writes=[("rstd", nt)])

    def norm_to_hb(self, gname):
        s = self.s
        g = self.gains[gname]
        self.rms_stats(self.xg, "xg")
        for nt in range(self.NT):
            sl = slice(nt * 512, (nt + 1) * 512)
            for c in range(NC8):
                s.op("vector", lambda e, c=c, sl=sl: e.scalar_tensor_tensor(out=self.hb[:, c, sl], in0=self.xg[:, c, sl], scalar=g[:, c:c + 1],
                                                                             in1=self.rstd[:, sl], op0=ALU.mult, op1=ALU.mult),
                     reads=[("xg", c, nt), ("rstd", nt), "g_" + gname], writes=[("hb", c, nt)])

    def evac_stats(self, bank, j, nt):
        s = self.s
        sl = slice(nt * 512, (nt + 1) * 512)
        s.op("scalar", lambda e: e.activation(out=self.yt[:, j, sl], in_=self.PS[bank][:], func=AF.Copy),
             reads=[("ps", bank)], writes=[("yt", j, nt)])
        qi = self.rot("sq", 2)
        sq = self.sq[qi]
        s.op("scalar", lambda e: e.activation(out=sq[:], in_=self.PS[bank][:], func=AF.Square),
             reads=[("ps", bank)], writes=[("sq", qi)])
        self.flush_stats()
        self.pending = (sq, qi, j, nt)

    def flush_stats(self):
        if getattr(self, "pending", None) is None:
            return
        sq, qi, j, nt = self.pending
        self.pending = None
        sb_ = 6 + nt
        self.s.op("tensor", lambda e: e.matmul(self.PS[sb_][:], lhsT=self.ones[:], rhs=sq[:], start=(j == 0), stop=(j == NC8 - 1)),
                  reads=["ones", ("sq", qi)], writes=[("ps", sb_)])

    def post_norm_residual(self, gname):
        s = self.s
        g = self.gains[gname]
        self.flush_stats()
        rs = self.rstd
        for nt in range(self.NT):
            sl = slice(nt * 512, (nt + 1) * 512)
            sb_ = 6 + nt
            s.op("scalar", lambda e, sb_=sb_, sl=sl: e.activation(out=rs[:, sl], in_=self.PS[sb_][:], func=AF.Ln, bias=self.epsc[:, 0:1], scale=1.0 / D),
                 reads=[("ps", sb_), "epsc"], writes=[("rstd", nt)])
            s.op("scalar", lambda e, sl=sl: e.activation(out=rs[:, sl], in_=rs[:, sl], func=AF.Exp, scale=-0.5), reads=[("rstd", nt)], writes=[("rstd", nt)])
        for nt in range(self.NT):
            sl = slice(nt * 512, (nt + 1) * 512)
            for c in range(NC8):
                s.op("vector", lambda e, c=c, sl=sl: e.scalar_tensor_tensor(out=self.yt[:, c, sl], in0=self.yt[:, c, sl], scalar=g[:, c:c + 1],
                                                                             in1=self.rstd[:, sl], op0=ALU.mult, op1=ALU.mult),
                     reads=[("yt", c, nt), ("rstd", nt), "g_" + gname], writes=[("yt", c, nt)])
                s.op("vector", lambda e, c=c, sl=sl: e.tensor_tensor(out=self.xg[:, c, sl], in0=self.xg[:, c, sl], in1=self.yt[:, c, sl], op=ALU.add),
                     reads=[("yt", c, nt), ("xg", c, nt)], writes=[("xg", c, nt)])

    def load_w(self, dram_ap, nk):
        s = self.s
        if nk <= NC8:
            i = self.rot("wA", 8)
            t = self.wA[i]
            key = ("wA", i)
        else:
            i = self.rot("wD", 3)
            t = self.wD[i]
            key = ("wD", i)
        s.dma("gpsimd", t[:, 0:nk, :], dram_ap, writes=[key])
        return t, key

    def mm_group(self, bank, wt, wkey, nk, rhs_fn, rhs_keys):
        ps = self.PS[bank]

        def fn(e):
            ins = None
            for kc in range(nk):
                ins = e.matmul(ps[:], lhsT=wt[:, kc, :], rhs=rhs_fn(kc), start=(kc == 0), stop=(kc == nk - 1))
            return ins
        return self.s.op("tensor", fn, reads=[wkey] + list(rhs_keys), writes=[("ps", bank)])

    def ffn(self, W, g_pre, g_post_half):
        s = self.s
        wg, wu, wd = W
        self.norm_to_hb(g_pre)
        for fc in range(NF):
            wgt, wgk = self.load_w(wg[fc], NC8)
            wut, wuk = self.load_w(wu[fc], NC8)
            for nt in range(self.NT):
                sl = slice(nt * 512, (nt + 1) * 512)
                par = self.rot("gu", 2)
                bg, bu = 0 + par, 2 + par
                hkeys = [("hb", c, nt) for c in range(NC8)]
                self.mm_group(bg, wgt, wgk, NC8, lambda kc, sl=sl: self.hb[:, kc, sl], hkeys)
                self.mm_group(bu, wut, wuk, NC8, lambda kc, sl=sl: self.hb[:, kc, sl], hkeys)
                sgi = self.rot("sg", 2)
                sg = self.sg[sgi]
                s.op("scalar", lambda e, sg=sg, bg=bg: e.activation(out=sg[:], in_=self.PS[bg][:], func=AF.Silu),
                     reads=[("ps", bg)], writes=[("sg", sgi)])
                s.op("vector", lambda e, sg=sg, bu=bu, fc=fc, sl=sl: e.tensor_tensor(out=self.act[:, fc, sl], in0=sg[:], in1=self.PS[bu][:], op=ALU.mult),
                     reads=[("sg", sgi), ("ps", bu)], writes=[("act", fc, nt)])
        for j in range(NC8):
            wdt, wdk = self.load_w(wd[j], NF)
            for nt in range(self.NT):
                sl = slice(nt * 512, (nt + 1) * 512)
                bank = 4 + self.rot("dn", 2)
                self.mm_group(bank, wdt, wdk, NF, lambda kc, sl=sl: self.act[:, kc, sl], [("act", fc, nt) for fc in range(NF)])
                self.evac_stats(bank, j, nt)
        self.post_norm_residual(g_post_half)

    def proj(self, P, g_pre, t0):
        s = self.s
        TG, NT = self.TG, self.NT
        if not self.proj_init:
            self.proj_init = True
            self.cos = s.sb("cos", [128, TG], F32)
            self.sin = s.sb("sin", [128, TG], F32)
            self.pm = s.sb("pm", [128, 128], BF16)
            s.dma("sync", self.pm[:], P["pm"], writes=["pm"])
            self.wV = s.sb("wV", [128, NC8, 512], BF16)
            self.stb = [s.sb("stb%d" % i, [128, 512], BF16) for i in range(3)]
            self.stf = [s.sb("stf%d" % i, [128, 512], F32) for i in range(3)]
        s.dma("sync", self.cos[:], P["cos"][:, t0:t0 + TG], writes=["cos"])
        s.dma("sync", self.sin[:], P["sin"][:, t0:t0 + TG], writes=["sin"])
        self.norm_to_hb(g_pre)
        for vg in range(3):
            s.dma("gpsimd", self.wV[:], P["w_v"][vg], writes=["wV"])
            for tt in range(TG // 128):
                tsl = slice(tt * 128, (tt + 1) * 128)
                cc = t0 // 128 + tt
                bank = 4 + self.rot("pv", 2)
                ps = self.PS[bank]

                def fn(e, ps=ps, tsl=tsl):
                    ins = None
                    for kc in range(NC8):
                        ins = e.matmul(ps[:], lhsT=self.hb[:, kc, tsl], rhs=self.wV[:, kc, :], start=(kc == 0), stop=(kc == NC8 - 1))
                    return ins
                s.op("tensor", fn, reads=["wV"] + [("hb", c, tt // 4) for c in range(NC8)], writes=[("ps", bank)])
                bi = self.rot("stb", 3)
                st = self.stb[bi]
                s.op("scalar", lambda e, st=st, ps=ps: e.activation(out=st[:], in_=ps[:], func=AF.Copy),
                     reads=[("ps", bank)], writes=[("stb", bi)])
                for hh in range(4):
                    dst = P["send"][vg * 4 + hh][128:256, cc * 128:(cc + 1) * 128]
                    d = s.dma("sync", dst, st[:, hh * 128:(hh + 1) * 128], reads=[("stb", bi)])
                    if vg == 0 and hh == 0:
                        P.setdefault("w0", []).append(d)
        ROPE = set(range(0, 4)) | set(range(8, 16)) | set(range(20, 24))
        for oc in list(range(12, 24)) + list(range(0, 12)) + list(range(24, 48)):
            if oc == 13 and t0 + TG >= TOK and P.get("early") is not None:
                P["early"]()
            wt, wk = self.load_w(P["w_fm"][oc], NC8)
            for nt in range(NT):
                sl = slice(nt * 512, (nt + 1) * 512)
                gsl = slice(t0 + nt * 512, t0 + (nt + 1) * 512)
                bank = self.rot("pj", 2)
                ps = self.PS[bank]
                hkeys = [("hb", c, nt) for c in range(NC8)]
                self.mm_group(bank, wt, wk, NC8, lambda kc, sl=sl: self.hb[:, kc, sl], hkeys)
                if oc >= 24:
                    fi = self.rot("stf", 3)
                    st = self.stf[fi]
                    s.op("scalar", lambda e, st=st, ps=ps: e.activation(out=st[:], in_=ps[:], func=AF.Sigmoid),
                         reads=[("ps", bank)], writes=[("stf", fi)])
                    s.dma("sync", P["g_dst"][oc - 24, :, gsl], st[:], reads=[("stf", fi)])
                    continue
                if oc < 12:
                    dst = P["q_dst"][oc, :, gsl]
                else:
                    dst = P["send"][oc - 12][0:128, gsl]
                if oc not in ROPE:
                    bi = self.rot("stb", 3)
                    st = self.stb[bi]
                    s.op("scalar", lambda e, st=st, ps=ps: e.activation(out=st[:], in_=ps[:], func=AF.Copy),
                         reads=[("ps", bank)], writes=[("stb", bi)])
                    s.dma("sync", dst, st[:], reads=[("stb", bi)])
                else:
                    bi = self.rot("stb", 3)
                    xb = self.stb[bi]
                    s.op("scalar", lambda e, xb=xb, ps=ps: e.activation(out=xb[:], in_=ps[:], func=AF.Copy),
                         reads=[("ps", bank)], writes=[("stb", bi)])
                    b2 = 2 + self.rot("pj2", 2)
                    ps2 = self.PS[b2]
                    s.op("tensor", lambda e, ps2=ps2, xb=xb: e.matmul(ps2[:], lhsT=self.pm[:], rhs=xb[:], start=True, stop=True),
                         reads=["pm", ("stb", bi)], writes=[("ps", b2)])
                    fi = self.rot("stf", 3)
                    t1 = self.stf[fi]
                    s.op("vector", lambda e, t1=t1, ps=ps, sl=sl: e.tensor_tensor(out=t1[:], in0=ps[:], in1=self.cos[:, sl], op=ALU.mult),
                         reads=[("ps", bank), "cos"], writes=[("stf", fi)])
                    fj = self.rot("stf", 3)
                    t2 = self.stf[fj]
                    s.op("vector", lambda e, t2=t2, ps2=ps2, sl=sl: e.tensor_tensor(out=t2[:], in0=ps2[:], in1=self.sin[:, sl], op=ALU.mult),
                         reads=[("ps", b2), "sin"], writes=[("stf", fj)])
                    bo = self.rot("stb", 3)
                    ob = self.stb[bo]
                    s.op("vector", lambda e, ob=ob, t1=t1, t2=t2: e.tensor_tensor(out=ob[:], in0=t1[:], in1=t2[:], op=ALU.add),
                         reads=[("stf", fi), ("stf", fj)], writes=[("stb", bo)])
                    d = s.dma("sync", dst, ob[:], reads=[("stb", bo)])
                    if oc == 12:
                        P.setdefault("w0", []).append(d)
    def merge(self, M, g_post, t0):
        s = self.s
        TG, NT = self.TG, self.NT
        if not self.merge_init:
            self.merge_init = True
            self.gt = [s.sb("gt%d" % i, [128, 512], F32) for i in range(3)]
            self.mt = [s.sb("mt%d" % i, [128, 512], F32) for i in range(2)]
        s.dma("sync", self.act[:, 0:12, :], M["a_src"][:, :, t0:t0 + TG].rearrange("c p t -> p c t"),
              writes=[("act", c, nt) for c in range(12) for nt in range(NT)])
        for j in range(NC8):
            wts = [self.load_w(M["w_mo"][i, j], 4) for i in range(3)]
            for nt in range(NT):
                sl = slice(nt * 512, (nt + 1) * 512)
                gsl = slice(t0 + nt * 512, t0 + (nt + 1) * 512)
                mi = self.rot("mt", 2)
                macc = self.mt[mi]
                for i in range(3):
                    wt, wk = wts[i]
                    bank = self.rot("mg", 3)
                    ps = self.PS[bank]
                    self.mm_group(bank, wt, wk, 4, lambda kc, sl=sl, i=i: self.act[:, 4 * i + kc, sl], [("act", 4 * i + kc, nt) for kc in range(4)])
                    gi = self.rot("gt", 3)
                    gt = self.gt[gi]
                    s.dma("sync", gt[:], M["g_src"][8 * i + j, :, gsl], writes=[("gt", gi)])
                    if i == 0:
                        s.op("vector", lambda e, macc=macc, ps=ps, gt=gt: e.tensor_tensor(out=macc[:], in0=ps[:], in1=gt[:], op=ALU.mult),
                             reads=[("ps", bank), ("gt", gi)], writes=[("mt", mi)])
                    else:
                        s.op("vector", lambda e, ps=ps, gt=gt: e.tensor_tensor(out=gt[:], in0=ps[:], in1=gt[:], op=ALU.mult),
                             reads=[("ps", bank), ("gt", gi)], writes=[("gt", gi)])
                        if i == 1:
                            s.op("vector", lambda e, macc=macc, gt=gt: e.tensor_tensor(out=macc[:], in0=macc[:], in1=gt[:], op=ALU.add),
                                 reads=[("mt", mi), ("gt", gi)], writes=[("mt", mi)])
                        else:
                            s.op("vector", lambda e, macc=macc, gt=gt, j=j, sl=sl: e.tensor_tensor(out=self.hb[:, j, sl], in0=macc[:], in1=gt[:], op=ALU.add),
                                 reads=[("mt", mi), ("gt", gi)], writes=[("hb", j, nt)])
        for j in range(NC8):
            wt, wk = self.load_w(M["w_out"][j], NC8)
            for nt in range(NT):
                sl = slice(nt * 512, (nt + 1) * 512)
                bank = 4 + self.rot("dn", 2)
                self.mm_group(bank, wt, wk, NC8, lambda kc, sl=sl: self.hb[:, kc, sl], [("hb", c, nt) for c in range(NC8)])
                self.evac_stats(bank, j, nt)
        self.post_norm_residual(g_post)

    def run(self, x_src, x_dst, merge=None, ffns=(), proj=None, final=None):
        s = self.s
        allx = [("xg", c, nt) for c in range(NC8) for nt in range(self.NT)]
        for g in range(TOK // self.TG):
            t0 = g * self.TG
            s.dma("sync", self.xg[:], x_src[:, :, t0:t0 + self.TG], writes=allx)
            if merge is not None:
                self.merge(merge[0], merge[1], t0)
            for W, gp, gq in ffns:
                self.ffn(W, gp, gq)
            if proj is not None:
                self.proj(proj[0], proj[1], t0)
            d = s.dma("sync", x_dst[:, :, t0:t0 + self.TG], self.xg[:], reads=allx)
            if final is not None:
                final.append(d)


class AttnStage:
    def __init__(self, ctx, q_loc, gath, o_dst, lam_aps):
        self.ctx = ctx
        self.nc, self.s, self.PS = ctx.nc, ctx.s, ctx.PS
        s = self.s
        self.q_loc, self.gath, self.o_dst = q_loc, gath, o_dst
        self.c = {}
        for name, (d, shape, dty) in ctx.acon_d.items():
            t = s.sb("c_" + name, shape, dty)
            s.dma("sync", t[:], d, writes=[name])
            self.c[name] = t
        self.sets = []
        for si in range(2):
            B = Ctx()
            B.QA = s.sb("QA%d" % si, [128, TOK], BF16)
            B.KA = s.sb("KA%d" % si, [128, S], BF16)
            B.KB = s.sb("KB%d" % si, [128, S], BF16)
            B.V = s.sb("V%d" % si, [128, NKC, 128], BF16)
            B.kQA, B.kKA, B.kKB, B.kV, B.si = ("QA", si), ("KA", si), ("KB", si), ("V", si), si
            s.op("gpsimd", lambda e, B=B: e.memset(B.KA[64:128, :], 0.0), writes=[B.kKA])
            s.op("gpsimd", lambda e, B=B: e.memset(B.KB[0:64, :], 0.0), writes=[B.kKB])
            self.sets.append(B)
        self.Pb = [s.sb("Pb%d" % i, [128, 512], BF16) for i in range(8)]
        self.E = [s.sb("E%d" % i, [128, 512], F32) for i in range(8)]
        self.sqb = s.sb("sqb", [128, 512], BF16)
        self.F = [s.sb("F%d" % i, [128, 512], F32) for i in range(6)]
        self.ob = [s.sb("ob%d" % i, [128, 512], BF16) for i in range(2)]
        self.carry = s.sb("carry", [128, 512], F32)
        self.carry2 = s.sb("carry2", [128, 512], F32)
        self.sm = s.sb("sm", [128, 64], F32)
        self.lam = s.sb("lam", [128, 8], F32)
        self.lamv = s.sb("lamv", [128, 4, 64], F32)
        self.lami = s.sb("lami", [128, 1], F32)
        self.dng = s.sb("dng", [128, 1], F32)
        s.dma("sync", self.lamv[:], lam_aps[0], writes=["lamv"])
        s.dma("sync", self.lami[:], lam_aps[1], writes=["lam_init"])
        s.dma("sync", self.dng[:], lam_aps[2], writes=["dng"])
        self.km = s.sb("km", [128, 32], F32)
        self.kmb = s.sb("kmb", [128, 32], BF16)
        self.gm = s.sb("gm", [128, 512], F32)
        self.btall = s.sb("btall", [128, 512], F32)
        self.mxall = s.sb("mxall", [128, 16, 8], F32)

        self.lam_setup()
        units = []
        for hh in range(4):
            units.append((lambda B, hh=hh: self.load_qkv(B, hh, hh, 0, hh), None, lambda B, hook, hh=hh: self.diff(B, hh, hook)))
        for hh in range(4):
            units.append((lambda B, hh=hh: self.load_qkv(B, 4 + hh, 4 + hh, 1, hh), None,
                          lambda B, hook, hh=hh: (self.sb2(B, hh) if hook is None else (self.sb(B, hh, 0, None), self.sb(B, hh, 1, hook)))))
        for hh in range(4):
            for h in range(2):
                units.append((lambda B, hh=hh, h=h: self.moba_load(B, hh, h), lambda B, hh=hh, h=h: self.moba_pre(B, hh, h),
                              lambda B, hook, hh=hh, h=h: self.moba(B, hh, h, hook)))
        units[0][0](self.sets[0])
        for n, (ld, pre, comp) in enumerate(units):
            hook = None
            if n + 1 < len(units):
                units[n + 1][0](self.sets[(n + 1) % 2])
                if units[n + 1][1] is not None:
                    hook = (lambda n=n: units[n + 1][1](self.sets[(n + 1) % 2]))
            comp(self.sets[n % 2], hook)

    def rot(self, name, n):
        return rot(self.ctx, name, n)

    def load_k(self, kchunk, dst_rows, src_rows, dst_t, key):
        s = self.s
        for r in range(4):
            base = r * 256
            src = self.gath[kchunk][base + src_rows.start:base + src_rows.stop, :].rearrange("p (i t) -> p i t", t=512)
            dst = dst_t[dst_rows, :].rearrange("p (i r t) -> p i r t", r=4, t=512)[:, :, r, :]
            s.dma("sync", dst, src, reads=[("gath", kchunk)], writes=[key])

    def load_v(self, B, vg, hh):
        s = self.s
        u = vg * 4 + hh
        for r in range(4):
            base = r * 256 + 128
            src = self.gath[u][base:base + 128, :].rearrange("p (i j d) -> p i j d", j=4, d=128)
            dst = B.V[:, :, :].rearrange("p (i r j) d -> p i r j d", r=4, j=4)[:, :, r, :, :]
            s.dma("sync", dst, src, reads=[("gath", u)], writes=[B.kV])

    def load_qkv(self, B, qchunk, kchunk_global, vg, hh):
        s = self.s
        s.dma("sync", B.QA[:, :], self.q_loc[qchunk, :, :], writes=[B.kQA])
        self.load_k(kchunk_global, slice(0, 64), slice(0, 64), B.KA, B.kKA)
        self.load_k(kchunk_global, slice(64, 128), slice(64, 128), B.KB, B.kKB)
        self.load_v(B, vg, hh)

    def moba_load(self, B, hh, h):
        s = self.s
        hp = slice(64 * h, 64 * h + 64)
        hi = slice(64, 128)
        s.op("gpsimd", lambda e: e.memset(B.KA[32:64, :], 0.0), writes=[B.kKA])
        s.op("gpsimd", lambda e: e.memset(B.QA[32:64, :], 0.0), writes=[B.kQA])
        s.dma("sync", B.KA[0:32, :], self.ctx.blk1h_d, writes=[B.kKA])
        s.dma("sync", B.QA[hi, :], self.q_loc[8 + hh, hp, :], writes=[B.kQA])
        self.load_k(8 + hh, hi, hp, B.KA, B.kKA)
        self.load_v(B, 2, hh)

    def lam_setup(self):
        s = self.s
        lamv, sm, lam = self.lamv, self.sm, self.lam
        s.op("vector", lambda e: e.tensor_tensor(out=sm[:, 0:64], in0=lamv[:, 0, :], in1=lamv[:, 1, :], op=ALU.mult), reads=["lamv"], writes=["sm"])
        s.op("vector", lambda e: e.reduce_sum(out=lam[:, 0:1], in_=sm[:, 0:64], axis=AX.X), reads=["sm"], writes=["lam0"])
        s.op("vector", lambda e: e.tensor_tensor(out=sm[:, 0:64], in0=lamv[:, 2, :], in1=lamv[:, 3, :], op=ALU.mult), reads=["lamv", "lam0"], writes=["sm"])
        s.op("vector", lambda e: e.reduce_sum(out=lam[:, 1:2], in_=sm[:, 0:64], axis=AX.X), reads=["sm"], writes=["lam1"])
        s.op("scalar", lambda e: e.activation(out=lam[:, 2:4], in_=lam[:, 0:2], func=AF.Exp), reads=["lam0", "lam1"], writes=["lam2"])
        s.op("vector", lambda e: e.tensor_tensor(out=lam[:, 4:5], in0=lam[:, 3:4], in1=lam[:, 2:3], op=ALU.subtract), reads=["lam2"], writes=["lam4"])
        s.op("vector", lambda e: e.tensor_tensor(out=lam[:, 5:6], in0=lam[:, 4:5], in1=self.lami[:, 0:1], op=ALU.subtract), reads=["lam4", "lam_init"], writes=["neglam"])
        s.op("vector", lambda e: e.tensor_tensor(out=lam[:, 6:7], in0=self.dng[:, 0:1], in1=self.lami[:, 0:1], op=ALU.mult), reads=["dng", "lam_init"], writes=["lam6"])
        s.op("vector", lambda e: e.tensor_tensor(out=lam[:, 7:8], in0=self.dng[:, 0:1], in1=lam[:, 6:7], op=ALU.subtract), reads=["dng", "lam6"], writes=["gsc"])

    def diff(self, B, hh, hook=None):
        s, c, PS, F = self.s, self.c, self.PS, self.F
        neglam = self.lam[:, 5:6]
        gsc = self.lam[:, 7:8]
        items = []
        for i in range(4):
            nch = 16 * i + 16
            for ch in range(nch):
                for m in range(2):
                    items.append(dict(i=i, ch=ch, m=m, nch=nch, m_=ch - 16 * i))

        def st_qk(it):
            i, ch, m = it["i"], it["ch"], it["m"]
            qs = slice(i * 512, (i + 1) * 512)
            ks = slice(ch * 128, (ch + 1) * 128)
            ps_ = slice(64 * m, 64 * m + 64)
            sb_ = it["sb"] = self.rot("dS", 4)
            kt = B.KA if m == 0 else B.KB
            s.op("tensor", lambda e: e.matmul(PS[sb_][:], lhsT=kt[:, ks], rhs=B.QA[:, qs], start=True, stop=True),
                 reads=[B.kQA, B.kKA, B.kKB], writes=[("ps", sb_)])

        def st_exp(it):
            sb_ = it["sb"]
            pi = it["pi"] = self.rot("Pb", 8)
            P = self.Pb[pi]
            s.op("scalar", lambda e: e.activation(out=P[:], in_=PS[sb_][:], func=AF.Exp, scale=0.125),
                 reads=[("ps", sb_)], writes=[("Pb", pi)])
            if it["m_"] >= 0:
                m_ = it["m_"]
                s.op("gpsimd", lambda e: e.tensor_tensor(out=P[:], in0=P[:], in1=c["mask_le"][:, m_, :], op=ALU.mult),
                     reads=[("Pb", pi), "mask_le"], writes=[("Pb", pi)])

        def st_pv(it):
            i, ch, m, nch, pi = it["i"], it["ch"], it["m"], it["nch"], it["pi"]
            P = self.Pb[pi]
            ob_, sb2 = 4 + 2 * m, 5 + 2 * m
            s.op("tensor", lambda e: e.matmul(PS[ob_][:], lhsT=B.V[:, ch, :], rhs=P[:], start=(ch == 0), stop=(ch == nch - 1)),
                 reads=[B.kV, ("Pb", pi)], writes=[("ps", ob_)])
            s.op("tensor", lambda e: e.matmul(PS[sb2][:], lhsT=c["ones_bf"][:], rhs=P[:], start=(ch == 0), stop=(ch == nch - 1)),
                 reads=["ones_bf", ("Pb", pi)], writes=[("ps", sb2)])
            if ch == nch - 1 and m == 1:
                finalize(i)

        def finalize(i):
            qs = slice(i * 512, (i + 1) * 512)
            s.op("vector", lambda e: e.reciprocal(out=F[0][:], in_=PS[5][:]), reads=[("ps", 5)], writes=[("F", 0)])
            s.op("vector", lambda e: e.reciprocal(out=F[1][:], in_=PS[7][:]), reads=[("ps", 7)], writes=[("F", 1)])
            s.op("vector", lambda e: e.tensor_tensor(out=F[2][:], in0=PS[4][:], in1=F[0][:], op=ALU.mult), reads=[("ps", 4), ("F", 0)], writes=[("F", 2)])
            s.op("vector", lambda e: e.scalar_tensor_tensor(out=F[3][:], in0=PS[6][:], scalar=neglam, in1=F[1][:], op0=ALU.mult, op1=ALU.mult),
                 reads=[("ps", 6), ("F", 1), "neglam"], writes=[("F", 3)])
            s.op("gpsimd", lambda e: e.tensor_tensor(out=F[2][:], in0=F[2][:], in1=F[3][:], op=ALU.add), reads=[("F", 2), ("F", 3)], writes=[("F", 2)])
            sqb = self.sqb
            s.op("scalar", lambda e: e.activation(out=sqb[:], in_=F[2][:], func=AF.Square), reads=[("F", 2)], writes=["sqb"])
            nb = self.rot("dS", 4)
            s.op("tensor", lambda e: e.matmul(PS[nb][:], lhsT=c["ones_bf"][:], rhs=sqb[:], start=True, stop=True), reads=["ones_bf", "sqb"], writes=[("ps", nb)])
            s.op("scalar", lambda e: e.activation(out=F[4][:], in_=PS[nb][:], func=AF.Ln, scale=1.0 / 128, bias=self.ctx.epsc[:, 0:1]), reads=[("ps", nb), "epsc"], writes=[("F", 4)])
            s.op("scalar", lambda e: e.activation(out=F[4][:], in_=F[4][:], func=AF.Exp, scale=-0.5), reads=[("F", 4)], writes=[("F", 4)])
            oi = self.rot("ob", 2)
            ob = self.ob[oi]
            s.op("vector", lambda e: e.scalar_tensor_tensor(out=ob[:], in0=F[2][:], scalar=gsc, in1=F[4][:], op0=ALU.mult, op1=ALU.mult),
                 reads=[("F", 2), ("F", 4), "gsc"], writes=[("ob", oi)])
            s.dma("sync", self.o_dst[hh, :, qs], ob[:], reads=[("ob", oi)])

        pipeline(items, [st_qk, st_exp, st_pv], [0, 1, 3], hook)

    def sb2(self, B, hh):
        s, c, PS, F = self.s, self.c, self.PS, self.F
        items = []
        for i in range(4):
            nch = 16 * i + 16
            for ch in range(nch - 1, -1, -1):
                for h in range(2):
                    items.append(dict(i=i, ch=ch, nch=nch, m_=ch - 16 * i, h=h))

        def st_qk(it):
            i, ch = it["i"], it["ch"]
            qs = slice(i * 512, (i + 1) * 512)
            ks = slice(ch * 128, (ch + 1) * 128)
            zb = it["zb"] = (0, 1)[self.rot("sZ2", 2)]
            kt = B.KA if it["h"] == 0 else B.KB
            s.op("tensor", lambda e: e.matmul(PS[zb][:], lhsT=kt[:, ks], rhs=B.QA[:, qs], start=True, stop=True),
                 reads=[B.kQA, B.kKA, B.kKB], writes=[("ps", zb)])

        def st_exp(it):
            zb = it["zb"]
            fi = it["fi"] = self.rot("sE", 8)
            ef = self.E[fi]
            s.op("scalar", lambda e: e.activation(out=ef[:], in_=PS[zb][:], func=AF.Exp, scale=0.125),
                 reads=[("ps", zb)], writes=[("E", fi)])

        def st_mask(it):
            if it["m_"] >= 0:
                fi, m_ = it["fi"], it["m_"]
                ef = self.E[fi]
                s.op("vector", lambda e: e.tensor_tensor(out=ef[:], in0=ef[:], in1=c["mask_lt"][:, m_, :], op=ALU.mult),
                     reads=[("E", fi), "mask_lt"], writes=[("E", fi)])

        def st_ln(it):
            fi = it["fi"]
            ef = self.E[fi]
            pi = it["pi"] = self.rot("Pb", 8)
            spb = self.Pb[pi]
            s.op("scalar", lambda e: e.activation(out=spb[:], in_=ef[:], func=AF.Ln, scale=1.0, bias=self.ctx.onec[:, 0:1]),
                 reads=[("E", fi), "onec"], writes=[("Pb", pi)])

        def st_tri(it):
            pi = it["pi"]
            spb = self.Pb[pi]
            lb = it["lb"] = 2 + self.rot("sL", 2)
            cb = it["cb"] = 4 + self.rot("sC", 2)
            s.op("tensor", lambda e: e.matmul(PS[lb][:], lhsT=c["tri"][:], rhs=spb[:], start=True, stop=True),
                 reads=["tri", ("Pb", pi)], writes=[("ps", lb)])
            s.op("tensor", lambda e: e.matmul(PS[cb][:], lhsT=c["ones_bf"][:], rhs=spb[:], start=True, stop=True),
                 reads=["ones_bf", ("Pb", pi)], writes=[("ps", cb)])

        def st_w(it):
            lb, cb, fi = it["lb"], it["cb"], it["fi"]
            ef = self.E[fi]
            h = it["h"]
            cy = self.carry if h == 0 else self.carry2
            ck = ("carry", h)
            if it["ch"] == it["nch"] - 1:
                s.op("gpsimd", lambda e: e.memset(cy[:], 0.0), writes=[ck])
            wi = it["wi"] = self.rot("sW", 4)
            wf = F[wi]
            s.op("vector", lambda e: e.tensor_tensor(out=wf[:], in0=PS[lb][:], in1=cy[:], op=ALU.add),
                 reads=[("ps", lb), ck], writes=[("F", wi)])
            if it["ch"] > 0:
                s.op("vector", lambda e: e.tensor_tensor(out=cy[:], in0=cy[:], in1=PS[cb][:], op=ALU.add),
                     reads=[ck, ("ps", cb)], writes=[ck])

        def st_g(it):
            wi = it["wi"]
            wf = F[wi]
            s.op("scalar", lambda e: e.activation(out=wf[:], in_=wf[:], func=AF.Exp, scale=-1.0), reads=[("F", wi)], writes=[("F", wi)])

        def st_a(it):
            fi, wi = it["fi"], it["wi"]
            ef, wf = self.E[fi], F[wi]
            ai = it["ai"] = self.rot("Pb", 8)
            A = self.Pb[ai]
            s.op("gpsimd", lambda e: e.tensor_tensor(out=A[:], in0=ef[:], in1=wf[:], op=ALU.mult), reads=[("E", fi), ("F", wi)], writes=[("Pb", ai)])

        def st_pv(it):
            i, ch, nch, ai, h = it["i"], it["ch"], it["nch"], it["ai"], it["h"]
            A = self.Pb[ai]
            ob_ = 6 + h
            hp = slice(64 * h, 64 * h + 64)
            s.op("tensor", lambda e: e.matmul(PS[ob_][:, :], lhsT=B.V[:, ch, :], rhs=A[:], start=(ch == nch - 1), stop=(ch == 0)),
                 reads=[B.kV, ("Pb", ai)], writes=[("ps", ob_)])
            if ch == 0:
                qs = slice(i * 512, (i + 1) * 512)
                oi = self.rot("ob", 2)
                ob = self.ob[oi]
                s.op("scalar", lambda e: e.activation(out=ob[hp, :], in_=PS[ob_][hp, :], func=AF.Copy), reads=[("ps", ob_)], writes=[("ob", oi)])
                s.dma("sync", self.o_dst[4 + hh, hp, qs], ob[hp, :], reads=[("ob", oi)])

        pipeline(items, [st_qk, st_exp, st_mask, st_ln, st_tri, st_w, st_g, st_a, st_pv], [0, 1, 2, 3, 4, 5, 6, 7, 8])


    def sb(self, B, hh, h, hook=None):
        s, c, PS, F = self.s, self.c, self.PS, self.F
        hp = slice(64 * h, 64 * h + 64)
        items = []
        for i in range(4):
            nch = 16 * i + 16
            for ch in range(nch - 1, -1, -1):
                items.append(dict(i=i, ch=ch, nch=nch, m_=ch - 16 * i))

        def st_qk(it):
            i, ch = it["i"], it["ch"]
            qs = slice(i * 512, (i + 1) * 512)
            ks = slice(ch * 128, (ch + 1) * 128)
            zb = it["zb"] = (0, 1, 7)[self.rot("sZ", 3)] if hook is None else (0, 1)[self.rot("sZ2", 2)]
            kt = B.KA if h == 0 else B.KB
            s.op("tensor", lambda e: e.matmul(PS[zb][:], lhsT=kt[:, ks], rhs=B.QA[:, qs], start=True, stop=True),
                 reads=[B.kQA, B.kKA, B.kKB], writes=[("ps", zb)])

        def st_exp(it):
            zb = it["zb"]
            fi = it["fi"] = self.rot("sE", 8)
            ef = self.E[fi]
            s.op("scalar", lambda e: e.activation(out=ef[:], in_=PS[zb][:], func=AF.Exp, scale=0.125),
                 reads=[("ps", zb)], writes=[("E", fi)])

        def st_mask(it):
            if it["m_"] >= 0:
                fi, m_ = it["fi"], it["m_"]
                ef = self.E[fi]
                s.op("vector", lambda e: e.tensor_tensor(out=ef[:], in0=ef[:], in1=c["mask_lt"][:, m_, :], op=ALU.mult),
                     reads=[("E", fi), "mask_lt"], writes=[("E", fi)])

        def st_ln(it):
            fi = it["fi"]
            ef = self.E[fi]
            pi = it["pi"] = self.rot("Pb", 8)
            spb = self.Pb[pi]
            s.op("scalar", lambda e: e.activation(out=spb[:], in_=ef[:], func=AF.Ln, scale=1.0, bias=self.ctx.onec[:, 0:1]),
                 reads=[("E", fi), "onec"], writes=[("Pb", pi)])

        def st_tri(it):
            pi = it["pi"]
            spb = self.Pb[pi]
            lb = it["lb"] = 2 + self.rot("sL", 2)
            cb = it["cb"] = 4 + self.rot("sC", 2)
            s.op("tensor", lambda e: e.matmul(PS[lb][:], lhsT=c["tri"][:], rhs=spb[:], start=True, stop=True),
                 reads=["tri", ("Pb", pi)], writes=[("ps", lb)])
            s.op("tensor", lambda e: e.matmul(PS[cb][:], lhsT=c["ones_bf"][:], rhs=spb[:], start=True, stop=True),
                 reads=["ones_bf", ("Pb", pi)], writes=[("ps", cb)])

        def st_w(it):
            lb, cb, fi = it["lb"], it["cb"], it["fi"]
            ef = self.E[fi]
            if it["ch"] == it["nch"] - 1:
                s.op("gpsimd", lambda e: e.memset(self.carry[:], 0.0), writes=["carry"])
            wi = it["wi"] = self.rot("sW", 4)
            wf = F[wi]
            s.op("vector", lambda e: e.tensor_tensor(out=wf[:], in0=PS[lb][:], in1=self.carry[:], op=ALU.add),
                 reads=[("ps", lb), "carry"], writes=[("F", wi)])
            if it["ch"] > 0:
                s.op("vector", lambda e: e.tensor_tensor(out=self.carry[:], in0=self.carry[:], in1=PS[cb][:], op=ALU.add),
                     reads=["carry", ("ps", cb)], writes=["carry"])

        def st_g(it):
            wi = it["wi"]
            wf = F[wi]
            s.op("scalar", lambda e: e.activation(out=wf[:], in_=wf[:], func=AF.Exp, scale=-1.0), reads=[("F", wi)], writes=[("F", wi)])

        def st_a(it):
            fi, wi = it["fi"], it["wi"]
            ef, wf = self.E[fi], F[wi]
            ai = it["ai"] = self.rot("Pb", 8)
            A = self.Pb[ai]
            s.op("gpsimd", lambda e: e.tensor_tensor(out=A[:], in0=ef[:], in1=wf[:], op=ALU.mult), reads=[("E", fi), ("F", wi)], writes=[("Pb", ai)])

        def st_pv(it):
            i, ch, nch, ai = it["i"], it["ch"], it["nch"], it["ai"]
            A = self.Pb[ai]
            s.op("tensor", lambda e: e.matmul(PS[6][:, :], lhsT=B.V[:, ch, :], rhs=A[:], start=(ch == nch - 1), stop=(ch == 0)),
                 reads=[B.kV, ("Pb", ai)], writes=[("ps", 6)])
            if ch == 0:
                qs = slice(i * 512, (i + 1) * 512)
                oi = self.rot("ob", 2)
                ob = self.ob[oi]
                s.op("scalar", lambda e: e.activation(out=ob[hp, :], in_=PS[6][hp, :], func=AF.Copy), reads=[("ps", 6)], writes=[("ob", oi)])
                s.dma("sync", self.o_dst[4 + hh, hp, qs], ob[hp, :], reads=[("ob", oi)])

        pipeline(items, [st_qk, st_exp, st_mask, st_ln, st_tri, st_w, st_g, st_a, st_pv], [0, 1, 2, 3, 4, 5, 6, 7, 8], hook)

    def moba_pre(self, B, hh, h):
        s, c, PS, F = self.s, self.c, self.PS, self.F
        hp = slice(64 * h, 64 * h + 64)
        hi = slice(64, 128)
        km, kmb = self.km, self.kmb
        s.op("vector", lambda e: e.tensor_reduce(out=km[hi, :], in_=B.KA[hi, :].rearrange("p (n k) -> p n k", k=256), axis=AX.X, op=ALU.add),
             reads=[B.kKA], writes=["km"])
        s.op("vector", lambda e: e.tensor_scalar(out=kmb[hi, :], in0=km[hi, :], scalar1=1.0 / 256, scalar2=None, op0=ALU.mult), reads=["km"], writes=["kmb"])
        VA = B.KB[:, :].rearrange("p (c d) -> p c d", d=128)
        s.op("gpsimd", lambda e: e.memset(VA[:, :, 64:128], 1.0), writes=[B.kKB])
        s.op("gpsimd", lambda e: e.tensor_copy(out=VA[:, :, 0:64], in_=B.V[:, :, hp]), reads=[B.kV], writes=[B.kKB])
        gm, bt, mx = self.gm, self.btall, self.mxall
        gbank = 7

        def gates(e):
            ins = None
            for qi in range(16):
                ins = e.matmul(PS[gbank][:, qi * 32:(qi + 1) * 32], lhsT=B.QA[hi, qi * 128:(qi + 1) * 128], rhs=kmb[hi, :], start=True, stop=True)
            return ins
        s.op("tensor", gates, reads=[B.kQA, "kmb"], writes=[("ps", gbank)])
        s.op("vector", lambda e: e.tensor_tensor(out=gm[:], in0=PS[gbank][:], in1=c["mvneg"][:].rearrange("p a b -> p (a b)"), op=ALU.add),
             reads=[("ps", gbank), "mvneg"], writes=["gm"])

        def maxes(e):
            ins = None
            for qi in range(16):
                ins = e.max(out=mx[:, qi, :], in_=gm[:, qi * 32:(qi + 1) * 32])
            return ins
        s.op("vector", maxes, reads=["gm"], writes=["mxall"])

        def sels(e):
            ins = None
            for qi in range(16):
                ins = e.tensor_scalar(out=bt[:, qi * 32:(qi + 1) * 32], in0=gm[:, qi * 32:(qi + 1) * 32], scalar1=mx[:, qi, 2:3], scalar2=None, op0=ALU.is_ge)
            return ins
        s.op("vector", sels, reads=["gm", "mxall"], writes=["btall"])
        s.op("vector", lambda e: e.tensor_tensor(out=bt[:], in0=bt[:], in1=c["mvalid"][:].rearrange("p a b -> p (a b)"), op=ALU.mult), reads=["btall", "mvalid"], writes=["btall"])
        s.op("vector", lambda e: e.tensor_tensor(out=bt[:], in0=bt[:], in1=c["mown"][:].rearrange("p a b -> p (a b)"), op=ALU.add), reads=["btall", "mown"], writes=["btall"])
        s.op("vector", lambda e: e.tensor_scalar(out=bt[:], in0=bt[:], scalar1=BIG, scalar2=-BIG, op0=ALU.mult, op1=ALU.add), reads=["btall"], writes=["btall"])
        for g4 in range(4):
            tb = 7

            def tr(e, g4=g4, tb=tb):
                ins = None
                for a in range(4):
                    qi = 4 * g4 + a
                    ins = e.transpose(PS[tb][0:32, a * 128:(a + 1) * 128], bt[:, qi * 32:(qi + 1) * 32], c["ident"][:])
                return ins
            s.op("tensor", tr, reads=["btall", "ident"], writes=[("ps", tb)])
            s.op("vector", lambda e, g4=g4, tb=tb: e.tensor_copy(out=B.QA[0:32, g4 * 512:(g4 + 1) * 512], in_=PS[tb][0:32, :]), reads=[("ps", tb)],
                 writes=[("QAb", B.si, 4 * g4 + a) for a in range(4)])

    def moba(self, B, hh, h, hook=None):
        s, c, PS, F = self.s, self.c, self.PS, self.F
        hp = slice(64 * h, 64 * h + 64)
        VA = B.KB[:, :].rearrange("p (c d) -> p c d", d=128)
        items = []
        for i in range(4):
            nch = 16 * i + 16
            for ch in range(nch):
                items.append(dict(i=i, ch=ch, nch=nch, m_=ch - 16 * i))

        def st_qk(it):
            i, ch = it["i"], it["ch"]
            qs = slice(i * 512, (i + 1) * 512)
            ks = slice(ch * 128, (ch + 1) * 128)
            sb_ = it["sb"] = self.rot("mS", 4)
            s.op("tensor", lambda e: e.matmul(PS[sb_][:], lhsT=B.KA[:, ks], rhs=B.QA[:, qs], start=True, stop=True),
                 reads=[B.kQA, B.kKA] + [("QAb", B.si, 4 * i + a) for a in range(4)], writes=[("ps", sb_)])

        def st_exp(it):
            sb_ = it["sb"]
            pi = it["pi"] = self.rot("Pb", 8)
            P = self.Pb[pi]
            s.op("scalar", lambda e: e.activation(out=P[:], in_=PS[sb_][:], func=AF.Exp, scale=0.125),
                 reads=[("ps", sb_)], writes=[("Pb", pi)])
            if it["m_"] >= 0:
                m_ = it["m_"]
                s.op("gpsimd", lambda e: e.tensor_tensor(out=P[:], in0=P[:], in1=c["mask_le"][:, m_, :], op=ALU.mult),
                     reads=[("Pb", pi), "mask_le"], writes=[("Pb", pi)])

        def st_pv(it):
            i, ch, nch, pi = it["i"], it["ch"], it["nch"], it["pi"]
            P = self.Pb[pi]
            ob_ = 4 + (i % 2)
            s.op("tensor", lambda e: e.matmul(PS[ob_][:], lhsT=VA[:, ch, :], rhs=P[:], start=(ch == 0), stop=(ch == nch - 1)),
                 reads=[B.kKB, ("Pb", pi)], writes=[("ps", ob_)])
            if ch == nch - 1:
                qs = slice(i * 512, (i + 1) * 512)
                ri = self.rot("mR", 2)
                s.op("vector", lambda e: e.reciprocal(out=F[ri][64:128, :], in_=PS[ob_][64:128, :]), reads=[("ps", ob_)], writes=[("F", ri)])
                s.dma("sync", F[2 + ri][0:64, :], F[ri][64:128, :], reads=[("F", ri)], writes=[("F", 2 + ri)])
                oi = self.rot("ob", 2)
                ob = self.ob[oi]
                s.op("vector", lambda e: e.tensor_tensor(out=ob[0:64, :], in0=PS[ob_][0:64, :], in1=F[2 + ri][0:64, :], op=ALU.mult),
                     reads=[("ps", ob_), ("F", 2 + ri)], writes=[("ob", oi)])
                s.dma("sync", self.o_dst[8 + hh, hp, qs], ob[0:64, :], reads=[("ob", oi)])

        pipeline(items, [st_qk, st_exp, st_pv], [0, 1, 3], hook)


def build_fused(n_layers=2):
    nc = bass.Bass("TRN2", target_bir_lowering=False)
    nc.allow_low_precision("bf16 matmul operands, fp32 accumulation")
    s = Sched(nc)
    ctx = Ctx()
    ctx.nc, ctx.s, ctx.cnt = nc, s, {}
    dt = nc.dram_tensor

    def din(name, shape, dty=F32):
        return dt(name, list(shape), dty, kind="ExternalInput").ap()
    x_in = din("x_in", [128, NC8, TOK])
    x_out = dt("x_out", [128, NC8, TOK], F32, kind="ExternalOutput").ap()
    ones_d = din("ones_bf", [128, 128], BF16)
    ctx.blk1h_d = din("blk1h", [32, S], BF16)
    ctx.acon_d = {}
    for name, shape, dty in (("mask_le", [128, 16, 512], BF16), ("mask_lt", [128, 16, 512], BF16), ("tri", [128, 128], BF16),
                             ("ones_bf", [128, 128], BF16), ("ident", [128, 128], F32),
                             ("mvneg", [128, 16, 32], F32), ("mvalid", [128, 16, 32], F32), ("mown", [128, 16, 32], F32)):
        d = ones_d if name == "ones_bf" else din(name, shape, dty)
        ctx.acon_d[name] = (d, shape, dty)
    cos_d = din("rope_cos", [128, TOK])
    sin_d = din("rope_sin", [128, TOK])
    pm_d = din("rope_perm", [128, 128], BF16)
    LW = []
    for l in range(n_layers):
        p = "l%d_" % l
        W = {}
        for f in ("ffn1", "ffn2"):
            W[f] = (din(p + f + "_wg", [NF, 128, NC8, 128]), din(p + f + "_wu", [NF, 128, NC8, 128]), din(p + f + "_wd", [NC8, 128, NF, 128]))
        for g in ("g_ffn1_pre", "g_ffn1_post", "g_mix_pre", "g_mix_post", "g_ffn2_pre", "g_ffn2_post"):
            W[g] = din(p + g, [128, NC8])
        W["w_fm"] = din(p + "w_in_fm", [48, 128, NC8, 128])
        W["w_v"] = din(p + "w_in_v", [3, 128, NC8, 512])
        W["w_mo"] = din(p + "w_mo", [3, NC8, 128, 4, 128])
        W["w_out"] = din(p + "w_out_t", [NC8, 128, NC8, 128])
        W["lamv"] = din(p + "lamv", [128, 4, 64])
        W["lam_init"] = din(p + "lam_init", [128, 1])
        W["dng"] = din(p + "dng", [128, 1])
        W["q_loc"] = dt(p + "q_loc", [12, 128, TOK], BF16).ap()
        W["g_scr"] = dt(p + "g_scr", [24, 128, TOK], F32).ap()
        W["o_loc"] = dt(p + "o_loc", [12, 128, TOK], BF16).ap()
        W["send"] = [dt(p + "send%d" % b, [256, TOK], BF16) for b in range(SEND_ROWS // 256)]
        W["gath"] = [dt(p + "gath%d" % b, [1024, TOK], BF16) for b in range(SEND_ROWS // 256)]
        LW.append(W)
    xbuf = dt("xbuf", [128, NC8, TOK], F32).ap()
    ctx.PS = [s.ps("ps%d" % i, [128, 512]) for i in range(8)]
    ctx.ones = s.sb("ones", [128, 128], BF16)
    s.dma("sync", ctx.ones[:], ones_d, writes=["ones"])
    ctx.epsc = s.sb("epsc", [128, 1], F32)
    s.op("gpsimd", lambda e: e.memset(ctx.epsc[:], EPS), writes=["epsc"])
    ctx.onec = s.sb("onec", [128, 1], F32)
    s.op("gpsimd", lambda e: e.memset(ctx.onec[:], 1.0), writes=["onec"])
    s.barrier()
    final = []

    def reinit_consts():
        pass

    def proj_P(W):
        P = dict(w_fm=W["w_fm"], w_v=W["w_v"], cos=cos_d, sin=sin_d, pm=pm_d, q_dst=W["q_loc"], send=[t.ap() for t in W["send"]], g_dst=W["g_scr"])

        def early(W=W, P=P):
            W["cc0"] = s.collective(lambda e, W=W: e.collective_compute("AllGather", ALU.bypass, replica_groups=[[0, 1, 2, 3], [4, 5, 6, 7]],
                                                                         ins=[W["send"][0].ap().opt()], outs=[W["gath"][0].ap().opt()]),
                                    deps=P["w0"])
        P["early"] = early
        return P

    def merge_M(W):
        return dict(a_src=W["o_loc"], g_src=W["g_scr"], w_mo=W["w_mo"], w_out=W["w_out"])

    for l in range(n_layers + 1):
        s.phase_begin()
        ts = TokStage(ctx)
        merge = None
        ffns = []
        proj = None
        if l > 0:
            Wp = LW[l - 1]
            ts.load_gain("mixpost", Wp["g_mix_post"])
            ts.load_gain("f2pre", Wp["g_ffn2_pre"])
            ts.load_gain("f2post", Wp["g_ffn2_post"], half=True)
            merge = (merge_M(Wp), "mixpost")
            ffns.append((Wp["ffn2"], "f2pre", "f2post"))
        if l < n_layers:
            W = LW[l]
            ts.load_gain("f1pre", W["g_ffn1_pre"])
            ts.load_gain("f1post", W["g_ffn1_post"], half=True)
            ts.load_gain("mixpre", W["g_mix_pre"])
            ffns.append((W["ffn1"], "f1pre", "f1post"))
            proj = (proj_P(W), "mixpre")
        last = (l == n_layers)
        ts.run(x_in if l == 0 else xbuf, x_out if last else xbuf, merge=merge, ffns=ffns, proj=proj, final=final if last else None)
        if last:
            s.join("sync", final)
            break
        s.phase_end()
        W = LW[l]
        s.last_w[("gath", 0)] = W["cc0"]
        s.readers[("gath", 0)] = []
        for b in range(1, 12):
            s.collective(lambda e, W=W, b=b: e.collective_compute("AllGather", ALU.bypass, replica_groups=[[0, 1, 2, 3], [4, 5, 6, 7]],
                                                                   ins=[W["send"][b].ap().opt()], outs=[W["gath"][b].ap().opt()]),
                         writes=[("gath", b)])
        s.phase_begin()
        AttnStage(ctx, W["q_loc"], [t.ap() for t in W["gath"]], W["o_loc"], (W["lamv"], W["lam_init"], W["dng"]))
        s.phase_end()
    s.emit()
    s.close()
    return nc, s


BF = ml_dtypes.bfloat16
NCORES = 8


def fm_in(w, nk, no):
    return np.ascontiguousarray(w.reshape(nk, 128, no, 128).transpose(2, 1, 0, 3))


def gain_t(g):
    return np.ascontiguousarray(g.reshape(8, 128).T)


def core_positions(c):
    t = c % 4
    return np.concatenate([np.arange((4 * i + t) * 512, (4 * i + t + 1) * 512) for i in range(4)])


def x_to_cores(x):
    out = []
    for c in range(NCORES):
        xc = x[c // 4][core_positions(c)]
        out.append(np.ascontiguousarray(xc.T.reshape(8, 128, 2048).transpose(1, 0, 2)))
    return out


def x_from_cores(xs):
    out = np.empty((2, 8192, 1024), np.float32)
    for c in range(NCORES):
        out[c // 4][core_positions(c)] = xs[c].transpose(1, 0, 2).reshape(1024, 2048).T
    return out


def rope_consts():
    inv_freq = (np.float32(500000.0) ** (-np.arange(0, 16, 2, dtype=np.float32) / np.float32(16))).astype(np.float32)
    pos = np.arange(8192, dtype=np.float32)
    ang = pos[:, None] * inv_freq[None, :]
    c = np.cos(ang).astype(np.float32)
    s = np.sin(ang).astype(np.float32)
    cosT = np.ones((128, 8192), np.float32)
    sinT = np.zeros((128, 8192), np.float32)
    pm = np.zeros((128, 128), np.float32)
    for p in range(128):
        r = p % 64
        if r < 8:
            cosT[p] = c[:, r]
            sinT[p] = -s[:, r]
            pm[p + 8, p] = 1.0
        elif r < 16:
            cosT[p] = c[:, r - 8]
            sinT[p] = s[:, r - 8]
            pm[p - 8, p] = 1.0
    return cosT, sinT, pm.astype(BF)


def core_consts(c, cosT, sinT):
    t = c % 4
    pos = core_positions(c)
    k = np.arange(128)[:, None, None]
    j = np.arange(4)[None, :, None]
    q = np.arange(512)[None, None, :]
    dle = ((128 * j + k) <= q).astype(np.float32)
    dlt = ((128 * j + k) < q).astype(np.float32)
    mle = np.zeros((128, 16, 512), np.float32)
    mlt = np.zeros((128, 16, 512), np.float32)
    for u in range(4):
        if u < t:
            mle[:, 4 * u:4 * u + 4] = 1.0
            mlt[:, 4 * u:4 * u + 4] = 1.0
        elif u == t:
            mle[:, 4 * u:4 * u + 4] = dle
            mlt[:, 4 * u:4 * u + 4] = dlt
    mvneg = np.zeros((16, 32), np.float32)
    mvalid = np.zeros((16, 32), np.float32)
    mown = np.zeros((16, 32), np.float32)
    for qi in range(16):
        i, jj = qi // 4, qi % 4
        own = ((4 * i + t) * 512 + jj * 128) // 256
        mvalid[qi, :own] = 1.0
        mvneg[qi, own:] = -30000.0
        mown[qi, own] = 1.0
    bc = lambda a: np.ascontiguousarray(np.broadcast_to(a[None], (128,) + a.shape))
    return {"rope_cos": np.ascontiguousarray(cosT[:, pos]), "rope_sin": np.ascontiguousarray(sinT[:, pos]),
            "mask_le": mle.astype(BF), "mask_lt": mlt.astype(BF), "mvneg": bc(mvneg), "mvalid": bc(mvalid), "mown": bc(mown)}


def shared_consts(pm):
    tri = (np.arange(128)[:, None] >= np.arange(128)[None, :]).astype(BF)
    blk = (np.arange(8192)[None, :] // 256 == np.arange(32)[:, None]).astype(BF)
    return {"tri": tri, "ones_bf": np.ones((128, 128), BF), "ident": np.eye(128, dtype=np.float32), "blk1h": blk, "rope_perm": pm}


_PROG = {}


def _proj_w(w_in):
    cols = [(0, 512), (1536, 2048), (3072, 3584), (512, 1024), (2048, 2560), (3584, 4096), (4608, 7680)]
    wq = np.concatenate([w_in[:, a:b] for a, b in cols], axis=1)
    wv = np.stack([np.ascontiguousarray(w_in[:, a:a + 512].reshape(8, 128, 512).transpose(1, 0, 2)) for a in (1024, 2560, 4096)])
    return fm_in(wq, 8, 48), wv


def kernel(x, w_in, w_diff_o, w_sb_o, w_moba_o, w_out, lam_q1, lam_k1, lam_q2, lam_k2,
           diff_norm_g, ffn1_wg, ffn1_wu, ffn1_wd, ffn2_wg, ffn2_wu, ffn2_wd,
           g_ffn1_pre, g_ffn1_post, g_mix_pre, g_mix_post, g_ffn2_pre, g_ffn2_post):
    f32 = lambda a: np.ascontiguousarray(np.asarray(a, dtype=np.float32))
    L = 2
    if "nc" not in _PROG:
        _PROG["nc"] = build_fused(L)[0]
    nc = _PROG["nc"]
    cosT, sinT, pm = rope_consts()
    com = shared_consts(pm)
    for l in range(L):
        p = "l%d_" % l
        for f, (wg, wu, wd) in (("ffn1", (ffn1_wg, ffn1_wu, ffn1_wd)), ("ffn2", (ffn2_wg, ffn2_wu, ffn2_wd))):
            com[p + f + "_wg"] = fm_in(f32(wg[l]), 8, 22)
            com[p + f + "_wu"] = fm_in(f32(wu[l]), 8, 22)
            com[p + f + "_wd"] = fm_in(f32(wd[l]), 22, 8)
        for name, g in (("g_ffn1_pre", g_ffn1_pre), ("g_ffn1_post", g_ffn1_post), ("g_mix_pre", g_mix_pre),
                        ("g_mix_post", g_mix_post), ("g_ffn2_pre", g_ffn2_pre), ("g_ffn2_post", g_ffn2_post)):
            com[p + name] = gain_t(f32(g[l]))
        com[p + "w_in_fm"], com[p + "w_in_v"] = _proj_w(f32(w_in[l]))
        com[p + "w_mo"] = np.stack([fm_in(f32(w_diff_o[l]), 4, 8), fm_in(f32(w_sb_o[l]), 4, 8), fm_in(f32(w_moba_o[l]), 4, 8)])
        com[p + "w_out_t"] = fm_in(f32(w_out[l]), 8, 8)
        lamv = np.stack([f32(lam_q1[l]), f32(lam_k1[l]), f32(lam_q2[l]), f32(lam_k2[l])])
        com[p + "lamv"] = np.ascontiguousarray(np.broadcast_to(lamv[None], (128, 4, 64)))
        com[p + "lam_init"] = np.full((128, 1), 0.8 - 0.6 * math.exp(-0.3 * l), np.float32)
        com[p + "dng"] = np.ascontiguousarray(f32(diff_norm_g[l]).reshape(128, 1))
    xs = x_to_cores(f32(x))
    in_maps = [dict(com, x_in=xs[c], **core_consts(c, cosT, sinT)) for c in range(NCORES)]
    res = run_bass_kernel_spmd(nc, in_maps, core_ids=list(range(NCORES)))
    return x_from_cores([res.results[c]["x_out"] for c in range(NCORES)])
```
